# Optimizing a Trainium2 kernel written in Bass

```python
import math
import jax, jax.numpy as jnp
from jax import lax
import numpy as np

D_MODEL = 1024
BATCH = 8
SEQ = 4096
DEPTH = 2

D_MIX = D_MODEL
N_MIXERS = 4
D_GROUP = D_MIX // N_MIXERS

MLA_HEADS = 4
MLA_Q_RANK = D_MODEL // 4
MLA_KV_RANK = D_MODEL // 8
MLA_NOPE = 64
MLA_ROPE = 32
MLA_V = D_GROUP // MLA_HEADS
ROPE_THETA = 10000.0

CONV_WIDTH = 3
CONV_CH = D_GROUP

POOL_WINDOWS = (2, 4, 8, 16)
POOL_GROUPS = len(POOL_WINDOWS)
POOL_CH = D_GROUP // POOL_GROUPS

SWA_HEADS = 4
SWA_KV_HEADS = 2
SWA_HEAD_DIM = D_GROUP // SWA_HEADS
SWA_WINDOW = 128

BLOCK = 128

D_FF = -(-8 * D_MODEL // (3 * 256)) * 256

RMS_EPS = 1e-6

IN_SPLITS = (MLA_Q_RANK, MLA_KV_RANK, MLA_ROPE,
             CONV_CH, CONV_CH, CONV_CH,
             D_GROUP,
             SWA_HEADS * SWA_HEAD_DIM,
             SWA_KV_HEADS * SWA_HEAD_DIM,
             SWA_KV_HEADS * SWA_HEAD_DIM)
D_IN = sum(IN_SPLITS)

kernel_name = "hybrid_parallel_mla_conv_pool_swa"


def _split_points():
    pts, acc = [], 0
    for w in IN_SPLITS[:-1]:
        acc += w
        pts.append(acc)
    return pts


def _alibi_slopes(n):
    return np.asarray([2.0 ** (-8.0 * (i + 1) / n) for i in range(n)], dtype=np.float32)


def rmsnorm(x, g):
    xf = x.astype(jnp.float32)
    y = xf * lax.rsqrt(jnp.mean(xf * xf, axis=-1, keepdims=True) + RMS_EPS)
    return (y * g.astype(jnp.float32)).astype(x.dtype)


def rope_tables(seq, dim, dtype):
    inv = 1.0 / (ROPE_THETA ** (jnp.arange(0, dim, 2, dtype=jnp.float32) / dim))
    ang = jnp.arange(seq, dtype=jnp.float32)[:, None] * inv[None, :]
    return jnp.cos(ang).astype(dtype), jnp.sin(ang).astype(dtype)


def apply_rope(x, cos, sin):
    x1, x2 = jnp.split(x, 2, axis=-1)
    c = cos[:, None, :]
    s = sin[:, None, :]
    return jnp.concatenate([x1 * c - x2 * s, x1 * s + x2 * c], axis=-1)


def mla_attention(c_q, c_kv, k_r, q_norm_g, kv_norm_g, w_uq, w_ukv, cos, sin):
    b, s, _ = c_q.shape
    dqk = MLA_NOPE + MLA_ROPE
    q = (rmsnorm(c_q, q_norm_g) @ w_uq).reshape(b, s, MLA_HEADS, dqk)
    q_nope, q_rot = q[..., :MLA_NOPE], q[..., MLA_NOPE:]
    q_rot = apply_rope(q_rot, cos, sin)
    kv = (rmsnorm(c_kv, kv_norm_g) @ w_ukv).reshape(b, s, MLA_HEADS, MLA_NOPE + MLA_V)
    k_nope, v = kv[..., :MLA_NOPE], kv[..., MLA_NOPE:]
    k_rot = apply_rope(k_r[:, :, None, :], cos, sin)
    k = jnp.concatenate([k_nope, jnp.broadcast_to(k_rot, (b, s, MLA_HEADS, MLA_ROPE))], axis=-1)
    q = jnp.concatenate([q_nope, q_rot], axis=-1) * (1.0 / math.sqrt(dqk))
    nb = s // BLOCK
    qb = q.reshape(b, nb, BLOCK, MLA_HEADS, dqk).transpose(1, 0, 2, 3, 4)
    key_pos = jnp.arange(s)

    def one_block(args):
        q_blk, i = args
        sc = jnp.einsum('bqhd,bkhd->bhqk', q_blk, k).astype(jnp.float32)
        q_pos = i * BLOCK + jnp.arange(BLOCK)
        causal = key_pos[None, :] <= q_pos[:, None]
        sc = jnp.where(causal[None, None], sc, -jnp.inf)
        p = jax.nn.softmax(sc, axis=-1).astype(v.dtype)
        return jnp.einsum('bhqk,bkhd->bqhd', p, v)

    out = lax.map(one_block, (qb, jnp.arange(nb)))
    return out.transpose(1, 0, 2, 3, 4).reshape(b, s, MLA_HEADS * MLA_V)


def short_gated_conv(gate_b, gate_c, u, conv_w):
    z = gate_c * u
    y = lax.conv_general_dilated(
        z, conv_w[:, None, :].astype(z.dtype), window_strides=(1,),
        padding=[(CONV_WIDTH - 1, 0)], dimension_numbers=('NWC', 'WIO', 'NWC'),
        feature_group_count=CONV_CH)
    return gate_b * y


def multiscale_pool(u, pool_w, pool_scale):
    b, s, _ = u.shape
    uf = u.astype(jnp.float32)
    cs = jnp.cumsum(uf, axis=1)
    pos = jnp.arange(s)
    outs = []
    for g, w in enumerate(POOL_WINDOWS):
        cs_g = cs[:, :, g * POOL_CH:(g + 1) * POOL_CH]
        lag = jnp.pad(cs_g, ((0, 0), (w, 0), (0, 0)))[:, :s]
        count = jnp.minimum(pos + 1, w).astype(jnp.float32)[None, :, None]
        outs.append((cs_g - lag) / count)
    pooled = jnp.stack(outs, axis=2) - uf.reshape(b, s, POOL_GROUPS, POOL_CH)
    mixed = jnp.einsum('bsgc,gcd->bsgd', pooled.astype(u.dtype), pool_w)
    return mixed.reshape(b, s, D_GROUP) * pool_scale


def swa_sink_attention(q, k, v, sinks, slopes):
    b, s, _, hd = q.shape
    grp = SWA_HEADS // SWA_KV_HEADS
    nb = s // BLOCK
    qb = q.reshape(b, nb, BLOCK, SWA_KV_HEADS, grp, hd)
    kb = k.reshape(b, nb, BLOCK, SWA_KV_HEADS, hd)
    vb = v.reshape(b, nb, BLOCK, SWA_KV_HEADS, hd)

    def with_prev(t):
        prev = jnp.pad(t, ((0, 0), (1, 0), (0, 0), (0, 0), (0, 0)))[:, :nb]
        return jnp.concatenate([prev, t], axis=2)

    kk, vv = with_prev(kb), with_prev(vb)
    sc = jnp.einsum('bnqkgd,bnskd->bnkgqs', qb, kk).astype(jnp.float32) * (1.0 / math.sqrt(hd))
    blk = jnp.arange(nb)[:, None] * BLOCK
    q_pos = blk + jnp.arange(BLOCK)[None, :]
    k_pos = blk - BLOCK + jnp.arange(2 * BLOCK)[None, :]
    dist = q_pos[:, :, None] - k_pos[:, None, :]
    valid = (dist >= 0) & (dist < SWA_WINDOW) & (k_pos[:, None, :] >= 0)
    sl = jnp.asarray(slopes).reshape(SWA_KV_HEADS, grp)
    bias = -sl[None, None, :, :, None, None] * dist.astype(jnp.float32)[None, :, None, None, :, :]
    sc = jnp.where(valid[None, :, None, None], sc + bias, -jnp.inf)
    sink = jnp.broadcast_to(
        sinks.astype(jnp.float32).reshape(SWA_KV_HEADS, grp)[None, None, :, :, None, None],
        sc.shape[:-1] + (1,))
    p = jax.nn.softmax(jnp.concatenate([sc, sink], axis=-1), axis=-1)[..., :-1].astype(v.dtype)
    out = jnp.einsum('bnkgqs,bnskd->bnqkgd', p, vv)
    return out.reshape(b, s, SWA_HEADS * hd)


def setup_inputs(seed: int = 0) -> dict:
    key = jax.random.key(seed)
    ks = jax.random.split(key, 17)
    f32 = jnp.float32

    def dense(k, shape, fan_in):
        return jax.random.normal(k, shape, f32) * fan_in ** -0.5

    def gain(k, shape):
        return 1.0 + 0.05 * jax.random.normal(k, shape, f32)

    return {
        "x": jax.random.normal(ks[0], (BATCH, SEQ, D_MODEL), f32),
        "attn_norm": gain(ks[1], (DEPTH, D_MODEL)),
        "w_in": dense(ks[2], (DEPTH, D_MODEL, D_IN), D_MODEL),
        "mla_q_norm": gain(ks[3], (DEPTH, MLA_Q_RANK)),
        "w_uq": dense(ks[4], (DEPTH, MLA_Q_RANK, MLA_HEADS * (MLA_NOPE + MLA_ROPE)), MLA_Q_RANK),
        "mla_kv_norm": gain(ks[5], (DEPTH, MLA_KV_RANK)),
        "w_ukv": dense(ks[6], (DEPTH, MLA_KV_RANK, MLA_HEADS * (MLA_NOPE + MLA_V)), MLA_KV_RANK),
        "conv_w": dense(ks[7], (DEPTH, CONV_WIDTH, CONV_CH), CONV_WIDTH),
        "pool_w": dense(ks[8], (DEPTH, POOL_GROUPS, POOL_CH, POOL_CH), POOL_CH),
        "pool_scale": gain(ks[9], (DEPTH, D_GROUP)),
        "swa_sinks": 0.5 * jax.random.normal(ks[10], (DEPTH, SWA_HEADS), f32),
        "mix_norm": gain(ks[11], (DEPTH, D_MIX)),
        "w_o": dense(ks[12], (DEPTH, D_MIX, D_MODEL), D_MIX),
        "ffn_norm": gain(ks[13], (DEPTH, D_MODEL)),
        "w_gate_up": dense(ks[14], (DEPTH, D_MODEL, 2 * D_FF), D_MODEL),
        "w_down": dense(ks[15], (DEPTH, D_FF, D_MODEL), D_FF),
        "final_norm": gain(ks[16], (D_MODEL,)),
    }


def reference(x, attn_norm, w_in, mla_q_norm, w_uq, mla_kv_norm, w_ukv, conv_w, pool_w,
              pool_scale, swa_sinks, mix_norm, w_o, ffn_norm, w_gate_up, w_down, final_norm):
    b, s, _ = x.shape
    cos, sin = rope_tables(s, MLA_ROPE, x.dtype)
    slopes = _alibi_slopes(SWA_HEADS)
    pts = _split_points()
    for l in range(DEPTH):
        h = rmsnorm(x, attn_norm[l])
        proj = h @ w_in[l]
        (c_q, c_kv, k_r, g_b, g_c, u_conv, u_pool,
         q_sw, k_sw, v_sw) = jnp.split(proj, pts, axis=-1)
        y_a = mla_attention(c_q, c_kv, k_r, mla_q_norm[l], mla_kv_norm[l],
                            w_uq[l], w_ukv[l], cos, sin)
        y_b = short_gated_conv(g_b, g_c, u_conv, conv_w[l])
        y_c = multiscale_pool(u_pool, pool_w[l], pool_scale[l])
        y_d = swa_sink_attention(q_sw.reshape(b, s, SWA_HEADS, SWA_HEAD_DIM),
                                 k_sw.reshape(b, s, SWA_KV_HEADS, SWA_HEAD_DIM),
                                 v_sw.reshape(b, s, SWA_KV_HEADS, SWA_HEAD_DIM),
                                 swa_sinks[l], slopes)
        groups = jnp.stack([y_a, y_b, y_c, y_d], axis=2)
        gf = groups.astype(jnp.float32)
        gf = gf * lax.rsqrt(jnp.mean(gf * gf, axis=-1, keepdims=True) + RMS_EPS)
        mixed = (gf.reshape(b, s, D_MIX) * mix_norm[l].astype(jnp.float32)).astype(x.dtype)
        x = x + mixed @ w_o[l]
        h2 = rmsnorm(x, ffn_norm[l])
        gate, up = jnp.split(h2 @ w_gate_up[l], 2, axis=-1)
        x = x + (jax.nn.silu(gate) * up) @ w_down[l]
    return rmsnorm(x, final_norm)
```

```python
import math
import os
import numpy as np
import concourse.bass as bass
import concourse.mybir as mybir
from concourse.bass_utils import run_bass_kernel_spmd

F32 = mybir.dt.float32
BF16 = mybir.dt.bfloat16
AF = mybir.ActivationFunctionType
ALU = mybir.AluOpType

L = 2
D = 1024
DFF = 2816
NSLAB = 94
VL = 39
NV = L * VL + 8
NCST = 1444
EPS = 1e-6
NWR = 8
ENGS = ("pe", "act", "dve", "pool", "sp")


class Tracker:
    def __init__(self, nc):
        self.nc = nc
        self.ops = {e: [] for e in ENGS}
        self.sems = {}
        self.cnt = {}
        self.waited = {e: {} for e in ENGS}
        self.lastw = {}
        self.readers = {}
        self.phase = 'pro'
        self.shapes = []
        self.tags = {e: [] for e in ENGS}
        for e in ENGS:
            self._sem("E_" + e)

    def _sem(self, name):
        if name not in self.sems:
            self.sems[name] = self.nc.alloc_semaphore(name=name)
            self.cnt[name] = 0
        return name

    def _deps(self, eng, reads, writes):
        need = {}
        own = "E_" + eng

        def add(tok, same_ok):
            s, v = tok
            if s == own and same_ok:
                return
            if v > need.get(s, 0):
                need[s] = v

        for k in reads:
            tok = self.lastw.get(k)
            if tok is not None:
                add(tok, eng == "pe")
        for k in writes:
            tok = self.lastw.get(k)
            if tok is not None:
                add(tok, eng == "pe")
            for s, v in self.readers.get(k, {}).items():
                add((s, v), eng == "pe")
        out = []
        w = self.waited[eng]
        for s, v in need.items():
            if w.get(s, 0) < v:
                w[s] = v
                out.append((s, v))
        return out

    def _emit_waits(self, eng, waits):
        for s, v in waits:
            h = self.sems[s]
            self.ops[eng].append(lambda e, h=h, v=v: e.wait_ge(h, v))

    def _record(self, tok, reads, writes):
        s, v = tok
        for k in reads:
            d = self.readers.setdefault(k, {})
            if d.get(s, 0) < v:
                d[s] = v
        for k in writes:
            self.lastw[k] = tok
            self.readers[k] = {}

    def op(self, eng, fn, reads=(), writes=(), signal=True):
        self._emit_waits(eng, self._deps(eng, reads, writes))
        self.tags[eng].append(self.phase)
        own = "E_" + eng
        if signal:
            self.cnt[own] += 1
            h = self.sems[own]
            self.ops[eng].append(lambda e, fn=fn, h=h: fn(e).then_inc(h, 1))
            tok = (own, self.cnt[own])
        else:
            self.ops[eng].append(lambda e, fn=fn: fn(e))
            tok = (own, self.cnt[own] + 1)
        self._record(tok, reads, writes)
        return tok

    def dma(self, eng, semname, out_ap, in_ap, reads=(), writes=()):
        self._sem(semname)
        waits = self._deps(eng, reads, writes)
        if self.cnt[semname] > self.waited[eng].get(semname, 0):
            self.waited[eng][semname] = self.cnt[semname]
            waits.append((semname, self.cnt[semname]))
        self._emit_waits(eng, waits)
        self.cnt[semname] += 16
        h = self.sems[semname]
        self.ops[eng].append(lambda e, o=out_ap, i=in_ap, h=h: e.dma_start(out=o, in_=i).then_inc(h, 16))
        tok = (semname, self.cnt[semname])
        self._record(tok, reads, writes)
        return tok

    def wait_all(self, eng):
        for s, v in self.cnt.items():
            if v > 0 and self.waited[eng].get(s, 0) < v:
                self.waited[eng][s] = v
                self._emit_waits(eng, [(s, v)])

    def replay(self):
        ops = self.ops
        with self.nc.Block() as block:
            @block.tensor
            def _(e):
                for f in ops["pe"]:
                    f(e)

            @block.scalar
            def _(e):
                for f in ops["act"]:
                    f(e)

            @block.vector
            def _(e):
                for f in ops["dve"]:
                    f(e)

            @block.gpsimd
            def _(e):
                for f in ops["pool"]:
                    f(e)

            @block.sync
            def _(e):
                for f in ops["sp"]:
                    f(e)


def build(NB):
    S = NB * 512
    nc = bass.Bass("TRN2", target_bir_lowering=False)
    xT_d = nc.dram_tensor("xT", [D, S], F32, kind="ExternalInput").ap()
    wpk_d = nc.dram_tensor("wpk", [L * NSLAB, 128, 1024], F32, kind="ExternalInput").ap()
    vecs_d = nc.dram_tensor("vecs", [128, NV], F32, kind="ExternalInput").ap()
    cst_d = nc.dram_tensor("cst", [128, NCST], F32, kind="ExternalInput").ap()
    rope_d = nc.dram_tensor("rope", [2, 32, S], F32, kind="ExternalInput").ap()
    out_d = nc.dram_tensor("outT", [D, S], F32, kind="ExternalOutput").ap()
    wbf_d = nc.dram_tensor("wbf", [L * NSLAB, 128, 1024], BF16, kind="Internal").ap()
    kt_d = nc.dram_tensor("ktd", [L, 96, 4, S], BF16, kind="Internal").ap()
    v_d = nc.dram_tensor("vd", [L, S // 128, 128, 384], BF16, kind="Internal").ap()

    xT_v = xT_d.rearrange("(c p) s -> p c s", p=128)
    out_v = out_d.rearrange("(c p) s -> p c s", p=128)

    A = nc.alloc_sbuf_tensor
    xTs = [A("xT_sb0", [128, 8, 512], F32), A("xT_sb1", [128, 8, 512], F32)]
    hb = A("hb", [128, 8, 512], BF16)
    act = A("act", [128, 22, 512], BF16)
    y = A("y", [128, 8, 512], F32)
    wring = [A(f"wr{i}", [128, 1024], BF16) for i in range(NWR)]
    rs = [A(f"rs{i}", [128, 512], F32) for i in range(2)]
    cq = A("cq", [128, 2, 512], F32)
    ckv = A("ckv", [128, 512], F32)
    cqn = A("cqn", [128, 2, 512], BF16)
    ckvn = A("ckvn", [128, 512], BF16)
    gb = A("gb", [128, 2, 512], F32)
    gc = A("gc", [128, 2, 512], F32)
    z = A("z", [128, 2, 514], F32)
    zh = [A(f"zh{l}", [128, 2, 2], F32) for l in range(L)]
    u = A("u", [128, 2, 528], F32)
    uh = [A(f"uh{l}", [128, 2, 16], F32) for l in range(L)]
    pa = A("pa", [128, 528], F32)
    pb = A("pb", [128, 528], F32)
    pooled = A("pooled", [128, 2, 512], BF16)
    qsw = A("qsw", [128, 2, 512], BF16)
    ksw = [A(f"ksw{l}", [128, 8 * 128], BF16) for l in range(L)]
    vsw = [A(f"vsw{l}", [128, 8, 384], BF16) for l in range(L)]
    qT = A("qT", [128, 4, 512], BF16)
    ktc = A("ktc", [128, 4, 512], BF16)
    vc = A("vc", [128, 4, 384], BF16)
    ktb = [A(f"ktb{i}", [128, 4, 512], BF16) for i in range(2)]
    vb = [A(f"vb{i}", [128, 4, 384], BF16) for i in range(2)]
    pt = [A(f"pt{i}", [128, 512], BF16) for i in range(3)]
    ctab = A("ctab", [128, 512], F32)
    stab = A("stab", [128, 512], F32)
    rt1 = A("rt1", [128, 512], F32)
    rt2 = A("rt2", [128, 512], F32)
    esw = [A(f"esw{i}", [128, 512], F32) for i in range(2)]
    psw = [A(f"psw{i}", [128, 512], BF16) for i in range(2)]
    rtt = A("rtt", [128, 512], F32)
    sg = [A(f"sg{i}", [128, 512], F32) for i in range(2)]
    vecs = A("vecs_sb", [128, NV], F32)
    cst = A("cst_sb", [128, NCST], F32)
    esink = A("esink", [128, L * 4], F32)
    ones_bf = A("ones_bf", [128, 128], BF16)
    poolw = A("poolw", [128, L, 256], BF16)
    swap_bf = A("swap_bf", [128, 128], BF16)
    rtb = [A(f"rtb{i}", [128, 512], BF16) for i in range(2)]
    tri_bf = A("tri_bf", [128, 128], BF16)
    ps = [nc.alloc_psum_tensor(f"ps{i}", [128, 512], F32) for i in range(8)]
    print("sbuf bytes remaining:", nc.sbuf_bytes_remaining)

    swapm = cst[:, 256:384]
    eps_col = cst[:, 1442:1443]
    zero_col = cst[:, 1443:1444]

    def swamask(h, c0, c1):
        return cst[:, 384 + h * 256 + c0: 384 + h * 256 + c1]

    t = Tracker(nc)
    cur = {"xT": xTs[0], "xi": 0}
    st = {"bank": 0, "wslab": 0, "kvl": 0, "pt": 0, "sw": 0, "sg": 0, "np": 0, "j": 0}

    held = set()

    def nextbank(hold=False):
        for _ in range(4):
            b = st["bank"] % 4
            st["bank"] += 1
            if b not in held:
                if hold:
                    held.add(b)
                return b
        raise RuntimeError("no free PSUM ring bank")

    def mm(out, lhsT, rhs, start, stop, reads, wkey):
        t.op("pe", lambda e: e.matmul(out, lhsT=lhsT, rhs=rhs, start=start, stop=stop),
             reads=reads, writes=[wkey], signal=stop)
        t.shapes.append((lhsT.shape[0], lhsT.shape[-1], rhs.shape[-1], str(lhsT.dtype)))

    def actf(out, in_, func, reads, writes, scale=1.0, bias=0.0):
        t.op("act", lambda e: e.activation(out=out, in_=in_, func=func, bias=bias, scale=scale),
             reads=reads, writes=writes)

    def tt(out, in0, in1, op, reads, writes, eng="dve"):
        t.op(eng, lambda e: e.tensor_tensor(out=out, in0=in0, in1=in1, op=op), reads=reads, writes=writes)

    def ts(out, in0, s1, op0, reads, writes, eng="dve"):
        t.op(eng, lambda e: e.tensor_scalar(out=out, in0=in0, scalar1=s1, scalar2=None, op0=op0),
             reads=reads, writes=writes)

    def stt(out, in0, scalar, in1, op0, op1, reads, writes, eng="dve"):
        t.op(eng, lambda e: e.scalar_tensor_tensor(out=out, in0=in0, scalar=scalar, in1=in1, op0=op0, op1=op1),
             reads=reads, writes=writes)

    def recip(out, in_, reads, writes):
        t.op("dve", lambda e: e.reciprocal(out=out, in_=in_), reads=reads, writes=writes)

    def cp(out, in_, reads, writes, eng="dve"):
        t.op(eng, lambda e: e.tensor_copy(out=out, in_=in_), reads=reads, writes=writes)

    def mset(ap, val, writes, eng="dve"):
        t.op(eng, lambda e: e.memset(ap, val), writes=writes)

    def load_slab(l, s):
        g = st["wslab"]
        st["wslab"] += 1
        slot = g % NWR
        if st["j"] == 0:
            t.dma("pool", f"WC{slot}", wring[slot][:, :], wpk_d[l * NSLAB + s], writes=[("wr", slot)])
            t.dma("sp", f"WS{g % 4}", wbf_d[l * NSLAB + s], wring[slot][:, :], reads=[("wr", slot)],
                  writes=[("wbf", l, s)])
        else:
            t.dma("sp", f"W{slot}", wring[slot][:, :], wbf_d[l * NSLAB + s],
                  reads=[("wbf", l, s)], writes=[("wr", slot)])
        return slot

    def stat_rstd(sq_list, Dn, rsbuf, rskey):
        b = nextbank()
        n = len(sq_list)
        for i, (ap, k) in enumerate(sq_list):
            mm(ps[b][:, :], ones_bf[:, :], ap, i == 0, i == n - 1, [k, "ones_bf"], ("ps", b))
        actf(rsbuf[:, :], ps[b][:, :], AF.Ln, [("ps", b)], [rskey], scale=1.0 / Dn, bias=eps_col)
        actf(rsbuf[:, :], rsbuf[:, :], AF.Exp, [rskey], [rskey], scale=-0.5)

    def rmsnorm_x(gbase, out_is_final=False):
        sql = []
        for c in range(8):
            actf(act[:, c, :], cur["xT"][:, c, :], AF.Square, [("x", cur["xi"], c)], [("act", c)])
            sql.append((act[:, c, :], ("act", c)))
        stat_rstd(sql, D, rs[0], "rs0")
        for c in range(8):
            if out_is_final:
                stt(y[:, c, :], cur["xT"][:, c, :], vecs[:, gbase + c: gbase + c + 1], rs[0][:, :], ALU.mult, ALU.mult,
                    [("x", cur["xi"], c), "rs0", "vecs"], [("y", c)])
            else:
                stt(hb[:, c, :], cur["xT"][:, c, :], vecs[:, gbase + c: gbase + c + 1], rs[0][:, :], ALU.mult, ALU.mult,
                    [("x", cur["xi"], c), "rs0", "vecs"], [("hb", c)])

    def mm_slab8(slot, mlo, mhi, bank, out_rows, extra_reads=()):
        for kc in range(8):
            mm(ps[bank][0:out_rows, :], wring[slot][:, kc * 128 + mlo: kc * 128 + mhi], hb[:, kc, :],
               kc == 0, kc == 7, [("wr", slot), ("hb", kc)], ("ps", bank))

    npk = {}

    def norm_pair_a(p, ychunk, biases):
        he, ho = 4 + 2 * p, 4 + 2 * p + 1
        yk = ("y", ychunk)
        k = st["np"] % 2
        st["np"] += 1
        npk[ychunk] = k
        actf(rtt[0:64, :], ps[ho][0:64, :], AF.Ln, [("ps", ho)], ["rtt"], bias=biases[0])
        actf(rtt[64:128, :], ps[he][64:128, :], AF.Ln, [("ps", he)], ["rtt"], bias=biases[1])
        actf(rtb[k][:, :], rtt[:, :], AF.Exp, ["rtt"], [("rtb", k)], scale=-1.0)
        cp(y[0:64, ychunk, :], ps[he][0:64, :], [("ps", he)], [yk])
        cp(y[64:128, ychunk, :], ps[ho][64:128, :], [("ps", ho)], [yk])

    def norm_pair_b(p, ychunk):
        yk = ("y", ychunk)
        k = npk[ychunk]
        b = nextbank()
        mm(ps[b][:, :], swap_bf[:, :], rtb[k][:, :], True, True, ["swap_bf", ("rtb", k)], ("ps", b))
        tt(y[:, ychunk, :], y[:, ychunk, :], ps[b][:, :], ALU.mult, [yk, ("ps", b)], [yk])

    def norm_pair(p, ychunk, biases):
        norm_pair_a(p, ychunk, biases)
        norm_pair_b(p, ychunk)

    t.dma("sp", "CST", cst[:, :], cst_d, writes=["cst"])
    t.dma("sp", "VEC", vecs[:, :], vecs_d, writes=["vecs"])
    t.dma("pool", "XLD0", xTs[0][:, :, :], xT_v[:, :, 0:512], writes=[("x", 0, c) for c in range(8)])
    cp(ones_bf[:, :], cst[:, 0:128], ["cst"], ["ones_bf"])
    cp(tri_bf[:, :], cst[:, 128:256], ["cst"], ["tri_bf"])
    cp(swap_bf[:, :], cst[:, 256:384], ["cst"], ["swap_bf"])
    for l in range(L):
        actf(esink[:, l * 4:(l + 1) * 4], vecs[:, l * VL + 35: l * VL + 39], AF.Exp, ["vecs"], ["esink"])
        mset(zh[l][:, :, :], 0.0, [("zh", l)])
        mset(uh[l][:, :, :], 0.0, [("uh", l)])
        mset(vsw[l][:, :, :], 1.0, [("vsw", l)])
    mset(vc[:, :, :], 1.0, ["vc"])

    scale_mla = 1.0 / math.sqrt(96.0)

    INCR = os.environ.get('MK_INCR', '0') == '1'
    HOIST = os.environ.get('MK_INJ', '1') == '1'
    CP_ENG = os.environ.get('MK_CPENG', 'dve')
    INJ0 = int(os.environ.get('MK_INJ0', '9'))
    SWLA = int(os.environ.get('MK_SWLA', '1'))
    CPINJ = os.environ.get('MK_CPINJ', '1') == '1'
    ILV = int(os.environ.get('MK_ILV', '1'))
    INJD = int(os.environ.get('MK_INJD', '0'))
    FFNRS = os.environ.get('MK_FFNRS', '1') == '1'
    GN2 = os.environ.get('MK_GN2', '1') == '1'
    NP2 = os.environ.get('MK_NP2', '0') == '1'
    WOSPLIT = os.environ.get('MK_WOSPLIT', '1') == '1'
    ILV_A0 = int(os.environ.get('MK_ILVA0', '6'))
    NPMID = os.environ.get('MK_NPMID', '1') == '1'
    HOISTPREP = os.environ.get('MK_HOISTPREP', '0') == '1'
    SB = 4

    def norm_sq(c, sq_ap, sq_key):
        actf(sq_ap, cur["xT"][:, c, :], AF.Square, [("x", cur["xi"], c)], [sq_key])

    def norm_stat(c, sq_ap, sq_key, first, last):
        mm(ps[SB][:, :], ones_bf[:, :], sq_ap, first, last, [sq_key, "ones_bf"], ("ps", SB))

    def norm_sq_chunk(c, sq_ap, sq_key, first, last):
        norm_sq(c, sq_ap, sq_key)
        norm_stat(c, sq_ap, sq_key, first, last)

    def norm_finish(gbase, final=False):
        actf(rs[0][:, :], ps[SB][:, :], AF.Ln, [("ps", SB)], ["rs0"], scale=1.0 / D, bias=eps_col)
        actf(rs[0][:, :], rs[0][:, :], AF.Exp, ["rs0"], ["rs0"], scale=-0.5)
        for c in range(8):
            dst, dk = (y[:, c, :], ("y", c)) if final else (hb[:, c, :], ("hb", c))
            stt(dst, cur["xT"][:, c, :], vecs[:, gbase + c: gbase + c + 1], rs[0][:, :], ALU.mult, ALU.mult,
                [("x", cur["xi"], c), "rs0", "vecs"], [dk])

    def mm_slab8o(slot, mlo, mhi, bank, out_rows, order):
        for i, kc in enumerate(order):
            mm(ps[bank][0:out_rows, :], wring[slot][:, kc * 128 + mlo: kc * 128 + mhi], hb[:, kc, :],
               i == 0, i == len(order) - 1, [("wr", slot), ("hb", kc)], ("ps", bank))

    for j in range(NB):
        tok0 = j * 512
        st["j"] = j
        cur["xi"] = j % 2
        cur["xT"] = xTs[j % 2]
        if j + 1 < NB:
            nx = (j + 1) % 2
            t.dma("pool", f"XLD{nx}", xTs[nx][:, :, :], xT_v[:, :, tok0 + 512:tok0 + 1024],
                  writes=[("x", nx, c) for c in range(8)])
        t.dma("sp", "ROPE", ctab[64:96, :], rope_d[0, :, tok0:tok0 + 512], writes=["ctab"])
        t.dma("sp", "ROPE", stab[64:96, :], rope_d[1, :, tok0:tok0 + 512], writes=["stab"])
        for l in range(L):
            vb0 = l * VL
            par8 = (j % 2) * 4
            t.phase = 'p1_norm'
            if l == 0 or not INCR:
                for c in range(8):
                    norm_sq_chunk(c, hb[:, c, :], ("hb", c), c == 0, c == 7)
            norm_finish(vb0 + 0)
            t.phase = 'p2_win'
            for c in range(2):
                slot = load_slab(l, c); b = nextbank()
                mm_slab8(slot, 0, 128, b, 128)
                actf(cq[:, c, :], ps[b][:, :], AF.Copy, [("ps", b)], [("cq", c)])
                actf(act[:, c, :], cq[:, c, :], AF.Square, [("cq", c)], [("act", c)])
            slot = load_slab(l, 2); b = nextbank()
            mm_slab8(slot, 0, 128, b, 128)
            actf(ckv[:, :], ps[b][:, :], AF.Copy, [("ps", b)], ["ckv"])
            actf(act[:, 2, :], ckv[:, :], AF.Square, ["ckv"], [("act", 2)])
            slot = load_slab(l, 3)
            bA = nextbank()
            mm_slab8(slot, 0, 96, bA, 96)
            bB = nextbank()
            mm_slab8(slot, 32, 128, bB, 96)
            tt(rt1[64:96, :], ps[bA][64:96, :], ctab[64:96, :], ALU.mult, [("ps", bA), "ctab"], ["rt1"])
            tt(rt2[64:96, :], ps[bB][64:96, :], stab[64:96, :], ALU.mult, [("ps", bB), "stab"], ["rt2"])
            for h in range(4):
                tt(ktc[64:96, h, :], rt1[64:96, :], rt2[64:96, :], ALU.add, ["rt1", "rt2"], [("ktc", h)])
            stat_rstd([(act[:, c, :], ("act", c)) for c in range(2)], 256.0, rs[1], "rs1")
            for c in range(2):
                stt(cqn[:, c, :], cq[:, c, :], vecs[:, vb0 + 24 + c: vb0 + 25 + c], rs[1][:, :], ALU.mult, ALU.mult,
                    [("cq", c), "rs1", "vecs"], [("cqn", c)])
            stat_rstd([(act[:, 2, :], ("act", 2))], 128.0, rs[0], "rs0")
            stt(ckvn[:, :], ckv[:, :], vecs[:, vb0 + 26: vb0 + 27], rs[0][:, :], ALU.mult, ALU.mult,
                ["ckv", "rs0", "vecs"], ["ckvn"])
            for c in range(2):
                slot = load_slab(l, 4 + c); b = nextbank()
                mm_slab8(slot, 0, 128, b, 128)
                actf(gb[:, c, :], ps[b][:, :], AF.Copy, [("ps", b)], [("gb", c)])
            def conv_chunk(c, eng=CP_ENG):
                t.phase = 'p5_conv'
                w0 = vecs[:, vb0 + 27 + c * 3 + 0: vb0 + 27 + c * 3 + 1]
                w1 = vecs[:, vb0 + 27 + c * 3 + 1: vb0 + 27 + c * 3 + 2]
                w2 = vecs[:, vb0 + 27 + c * 3 + 2: vb0 + 27 + c * 3 + 3]
                yk = ("y", 2 + c)
                ts(y[:, 2 + c, :], z[:, c, 2:514], w2, ALU.mult, [("z", c), "vecs"], [yk], eng=eng)
                if eng == "pool":
                    for (zoff, wk) in ((1, w1), (0, w0)):
                        ts(sg[c][:, :], z[:, c, zoff:zoff + 512], wk, ALU.mult, [("z", c), "vecs"], [("sg", c)], eng=eng)
                        tt(y[:, 2 + c, :], y[:, 2 + c, :], sg[c][:, :], ALU.add, [yk, ("sg", c)], [yk], eng=eng)
                else:
                    stt(y[:, 2 + c, :], z[:, c, 1:513], w1, y[:, 2 + c, :], ALU.mult, ALU.add, [("z", c), yk, "vecs"], [yk], eng=eng)
                    stt(y[:, 2 + c, :], z[:, c, 0:512], w0, y[:, 2 + c, :], ALU.mult, ALU.add, [("z", c), yk, "vecs"], [yk], eng=eng)
                tt(y[:, 2 + c, :], y[:, 2 + c, :], gb[:, c, :], ALU.mult, [yk, ("gb", c)], [yk], eng=eng)

            def pool_chunk(c, eng=CP_ENG):
                t.phase = 'p6a_pool'
                uk = ("u", c)
                tt(pa[:, 1:528], u[:, c, 1:528], u[:, c, 0:527], ALU.add, [uk], ["pa"], eng=eng)
                tt(pb[:, 3:528], pa[:, 3:528], pa[:, 1:526], ALU.add, ["pa"], ["pb"], eng=eng)
                if c == 1:
                    tt(pa[:, 7:528], pb[:, 7:528], pb[:, 3:524], ALU.add, ["pb"], ["pa"], eng=eng)
                    tt(pb[:, 15:528], pa[:, 15:528], pa[:, 7:520], ALU.add, ["pa"], ["pb"], eng=eng)
                for (r0, r1, src, sk) in ((0, 64, pa, "pa"), (64, 128, pb, "pb")):
                    if j == 0:
                        tt(src[r0:r1, 16:32], src[r0:r1, 16:32], cst[r0:r1, 1410 + c * 16: 1410 + (c + 1) * 16],
                           ALU.mult, [sk, "cst"], [sk], eng=eng)
                    if eng == "pool":
                        ts(src[r0:r1, 16:528], src[r0:r1, 16:528], cst[r0:r1, 1408 + c: 1409 + c], ALU.mult,
                           [sk, "cst"], [sk], eng=eng)
                        tt(pooled[r0:r1, c, :], src[r0:r1, 16:528], u[r0:r1, c, 16:528], ALU.subtract,
                           [sk, uk], [("pooled", c)], eng=eng)
                    else:
                        stt(pooled[r0:r1, c, :], src[r0:r1, 16:528], cst[r0:r1, 1408 + c: 1409 + c], u[r0:r1, c, 16:528],
                            ALU.mult, ALU.subtract, [sk, uk, "cst"], [("pooled", c)], eng=eng)

            def _prep():
                t.phase = 'p3_mlaprep'
                slot = load_slab(l, 16)
                for h in range(4):
                    bA = nextbank()
                    for kc in range(2):
                        o = kc * 512 + h * 128
                        mm(ps[bA][0:96, :], wring[slot][:, o:o + 96], cqn[:, kc, :], kc == 0, kc == 1,
                           [("wr", slot), ("cqn", kc)], ("ps", bA))
                    bB = nextbank()
                    for kc in range(2):
                        o = kc * 512 + h * 128
                        mm(ps[bB][0:96, :], wring[slot][:, o + 32:o + 128], cqn[:, kc, :], kc == 0, kc == 1,
                           [("wr", slot), ("cqn", kc)], ("ps", bB))
                    actf(qT[0:64, h, :], ps[bA][0:64, :], AF.Copy, [("ps", bA)], [("qT", h)])
                    tt(rt1[64:96, :], ps[bA][64:96, :], ctab[64:96, :], ALU.mult, [("ps", bA), "ctab"], ["rt1"])
                    tt(rt2[64:96, :], ps[bB][64:96, :], stab[64:96, :], ALU.mult, [("ps", bB), "stab"], ["rt2"])
                    tt(qT[64:96, h, :], rt1[64:96, :], rt2[64:96, :], ALU.add, ["rt1", "rt2"], [("qT", h)])
                    yield
                slot17 = load_slab(l, 17)
                if j == 0:
                    cp(poolw[:, l, :], wring[slot17][:, 512:768], [("wr", slot17)], [("poolw", l)])
                for h in range(4):
                    b = nextbank()
                    mm(ps[b][0:64, :], wring[slot17][:, h * 64:(h + 1) * 64], ckvn[:, :], True, True,
                       [("wr", slot17), "ckvn"], ("ps", b))
                    actf(ktc[0:64, h, :], ps[b][0:64, :], AF.Copy, [("ps", b)], [("ktc", h)])
                    if h % 2 == 1:
                        yield
                for half in range(2):
                    b = nextbank()
                    for t2 in range(2):
                        tt_ = half * 2 + t2
                        mm(ps[b][:, t2 * 256:(t2 + 1) * 256], ckvn[:, tt_ * 128:(tt_ + 1) * 128],
                           wring[slot17][:, 256:512], True, True, [("wr", slot17), "ckvn"], ("ps", b))
                    psv = ps[b][:, :].rearrange("p (t c) -> p t c", c=256)
                    for h in range(4):
                        col = (h // 2) * 192 + (h % 2) * 128
                        cp(vc[:, half * 2:half * 2 + 2, col:col + 64], psv[:, :, h * 64:(h + 1) * 64],
                           [("ps", b)], ["vc"])
                    yield
                if j + 1 < NB:
                    t.dma("pool", f"KST{l}", kt_d[l][:, :, tok0:tok0 + 512], ktc[0:96, :, :],
                          reads=[("ktc", h) for h in range(4)], writes=[("ktd", l, j)])
                    t.dma("pool", f"VST{l}", v_d[l][4 * j:4 * j + 4].rearrange("t p c -> p t c"), vc[:, :, :],
                          reads=["vc"], writes=[("vd", l, j)])
            def _p2c():
                t.phase = 'p2_win'
                for c in range(2):
                    slot = load_slab(l, 6 + c); b = nextbank()
                    mm_slab8(slot, 0, 128, b, 128)
                    actf(gc[:, c, :], ps[b][:, :], AF.Copy, [("ps", b)], [("gc", c)])
                    yield
                for c in range(2):
                    cp(z[:, c, 0:2], zh[l][:, c, :], [("zh", l)], [("z", c)])
                    slot = load_slab(l, 8 + c); b = nextbank()
                    mm_slab8(slot, 0, 128, b, 128)
                    tt(z[:, c, 2:514], ps[b][:, :], gc[:, c, :], ALU.mult, [("ps", b), ("gc", c)], [("z", c)])
                    cp(zh[l][:, c, :], z[:, c, 512:514], [("z", c)], [("zh", l)])
                    yield
                for c in range(2):
                    cp(u[:, c, 0:16], uh[l][:, c, :], [("uh", l)], [("u", c)])
                    slot = load_slab(l, 10 + c); b = nextbank()
                    mm_slab8(slot, 0, 128, b, 128)
                    actf(u[:, c, 16:528], ps[b][:, :], AF.Copy, [("ps", b)], [("u", c)])
                    cp(uh[l][:, c, :], u[:, c, 512:528], [("u", c)], [("uh", l)])
                    yield
                if not CPINJ:
                    conv_chunk(0); conv_chunk(1); pool_chunk(0); pool_chunk(1)
                t.phase = 'p2_win'
                for c in range(2):
                    slot = load_slab(l, 12 + c); b = nextbank()
                    mm_slab8(slot, 0, 128, b, 128)
                    actf(qsw[:, c, :], ps[b][:, :], AF.Copy, [("ps", b)], [("qsw", c)])
                    yield
                slot = load_slab(l, 14); b = nextbank()
                mm_slab8(slot, 0, 128, b, 128)
                actf(ksw[l][:, par8 * 128:(par8 + 4) * 128], ps[b][:, :], AF.Copy, [("ps", b)], [("ksw", l, j % 2)])
                yield
                slot = load_slab(l, 15); b = nextbank()
                for tt_ in range(4):
                    for kc in range(8):
                        mm(ps[b][:, tt_ * 128:(tt_ + 1) * 128], hb[:, kc, tt_ * 128:(tt_ + 1) * 128],
                           wring[slot][:, kc * 128:(kc + 1) * 128], kc == 0, kc == 7,
                           [("wr", slot), ("hb", kc)], ("ps", b))
                psv = ps[b][:, :].rearrange("p (t c) -> p t c", c=128)
                for kv in range(2):
                    for off in (0, 128):
                        cp(vsw[l][:, par8:par8 + 4, kv * 192 + off: kv * 192 + off + 64],
                           psv[:, :, kv * 64:(kv + 1) * 64], [("ps", b)], [("vsw", l)])
            gens = [_p2c(), _prep()]
            if ILV == 0:
                for g_ in gens:
                    for _ in g_:
                        pass
            else:
                alive = [True, True]
                first_a = ILV_A0
                while any(alive):
                    for gi_, g_ in enumerate(gens):
                        n_ = first_a if (gi_ == 0 and first_a) else 1
                        if gi_ == 0:
                            first_a = 0
                        for _ in range(max(n_, 1)):
                            if alive[gi_]:
                                try:
                                    next(g_)
                                except StopIteration:
                                    alive[gi_] = False
            t.phase = 'p7_swa'
            sw_steps = [(p, qt) for p in range(2) for qt in range(4)]
            sw_bank = {}

            def swS(i):
                p, qt = sw_steps[i]
                T = 4 * j + qt
                cur_s = T % 8
                prv_s = (T - 1) % 8
                r0 = p * 64
                b = nextbank(hold=True)
                sw_bank[i] = b
                for hh in range(2):
                    rk = [("ksw", l, 0), ("ksw", l, 1), ("qsw", hh)]
                    if T > 0:
                        mm(ps[b][:, hh * 256:hh * 256 + 128], ksw[l][r0:r0 + 64, prv_s * 128:(prv_s + 1) * 128],
                           qsw[r0:r0 + 64, hh, qt * 128:(qt + 1) * 128], True, True, rk, ("ps", b))
                    mm(ps[b][:, hh * 256 + 128:hh * 256 + 256], ksw[l][r0:r0 + 64, cur_s * 128:(cur_s + 1) * 128],
                       qsw[r0:r0 + 64, hh, qt * 128:(qt + 1) * 128], True, True, rk, ("ps", b))

            def swRest(i):
                p, qt = sw_steps[i]
                T = 4 * j + qt
                cur_s = T % 8
                prv_s = (T - 1) % 8
                b = sw_bank[i]
                si = st["sw"] % 2
                st["sw"] += 1
                rngs = [(0, 512)] if T > 0 else [(128, 256), (384, 512)]
                for (c0, c1) in rngs:
                    actf(esw[si][:, c0:c1], ps[b][:, c0:c1], AF.Exp, [("ps", b)], [("esw", si)], scale=0.125)
                    tt(psw[si][:, c0:c1], esw[si][:, c0:c1], cst[:, 384 + p * 512 + c0: 384 + p * 512 + c1], ALU.mult,
                       [("esw", si), "cst"], [("psw", si)])
                held.discard(b)
                for hh in range(2):
                    h = 2 * p + hh
                    vcol = p * 192 + hh * 64
                    ob = 4 + h
                    if T > 0:
                        mm(ps[ob][:, qt * 128:(qt + 1) * 128], vsw[l][:, prv_s, vcol:vcol + 128],
                           psw[si][:, hh * 256:hh * 256 + 128], True, False, [("vsw", l), ("psw", si)], ("ps", ob))
                    mm(ps[ob][:, qt * 128:(qt + 1) * 128], vsw[l][:, cur_s, vcol:vcol + 128],
                       psw[si][:, hh * 256 + 128:hh * 256 + 256], not (T > 0), True, [("vsw", l), ("psw", si)], ("ps", ob))

            for i0_ in range(SWLA):
                swS(i0_)
            for i in range(len(sw_steps)):
                if i + SWLA < len(sw_steps):
                    swS(i + SWLA)
                swRest(i)
                if i == 3 and NPMID:
                    norm_pair_a(0, 6, [esink[0:64, l * 4 + 1: l * 4 + 2], esink[64:128, l * 4 + 0: l * 4 + 1]])
                if i == 5 and NPMID:
                    norm_pair_b(0, 6)
            if not NPMID:
                norm_pair(0, 6, [esink[0:64, l * 4 + 1: l * 4 + 2], esink[64:128, l * 4 + 0: l * 4 + 1]])
            if NP2:
                norm_pair_a(1, 7, [esink[0:64, l * 4 + 3: l * 4 + 4], esink[64:128, l * 4 + 2: l * 4 + 3]])
            else:
                norm_pair(1, 7, [esink[0:64, l * 4 + 3: l * 4 + 4], esink[64:128, l * 4 + 2: l * 4 + 3]])

            def poolmm(c):
                t.phase = 'p6b_poolmm'
                b = nextbank()
                mm(ps[b][:, :], poolw[:, l, c * 128:(c + 1) * 128], pooled[:, c, :], True, True,
                   [("poolw", l), ("pooled", c)], ("ps", b))
                ts(y[:, 4 + c, :], ps[b][:, :], vecs[:, vb0 + 33 + c: vb0 + 34 + c], ALU.mult,
                   [("ps", b), "vecs"], [("y", 4 + c)])

            gbank = {}

            def gnorm_sq(g):
                t.phase = 'p8_gnorm'
                for c in range(2):
                    cc = 2 * g + c
                    tt(act[:, cc, :], y[:, cc, :], y[:, cc, :], ALU.mult, [("y", cc)], [("act", cc)])

            def gnorm_st(g):
                t.phase = 'p8_gnorm'
                b = nextbank(hold=True)
                gbank[g] = b
                for c in range(2):
                    cc = 2 * g + c
                    mm(ps[b][:, :], ones_bf[:, :], act[:, cc, :], c == 0, c == 1, [("act", cc), "ones_bf"], ("ps", b))

            def gnorm_a(g):
                gnorm_sq(g)
                gnorm_st(g)

            def gnorm_b(g):
                t.phase = 'p8_gnorm'
                b = gbank[g]
                rsb, rsk = (rs[g % 2], f"rs{g % 2}")
                actf(rsb[:, :], ps[b][:, :], AF.Ln, [("ps", b)], [rsk], scale=1.0 / 256.0, bias=eps_col)
                held.discard(b)
                actf(rsb[:, :], rsb[:, :], AF.Exp, [rsk], [rsk], scale=-0.5)
                for c in range(2):
                    cc = 2 * g + c
                    stt(hb[:, cc, :], y[:, cc, :], vecs[:, vb0 + 16 + cc: vb0 + 17 + cc], rsb[:, :], ALU.mult, ALU.mult,
                        [("y", cc), rsk, "vecs"], [("hb", cc)])

            def gnorm(g):
                gnorm_a(g)
                gnorm_b(g)

            if GN2:
                inject = [(27, lambda: gnorm_sq(1)), (29, lambda: poolmm(0)), (31, lambda: gnorm_st(1)), (33, lambda: gnorm_sq(3)),
                          (35, lambda: gnorm_b(1)), (37, lambda: gnorm_st(3)), (39, lambda: poolmm(1)), (41, lambda: gnorm_b(3)),
                          (43, lambda: poolmm(1) if False else None), (45, lambda: gnorm_sq(2)), (47, lambda: gnorm_st(2)),
                          (51, lambda: gnorm_b(2))]
                inject = [e for e in inject if e[0] != 43]
            else:
                inject = [(29 + INJD, lambda: poolmm(0)), (39 + INJD, lambda: poolmm(1)), (33 + INJD, lambda: gnorm(1)),
                          (41 + INJD, lambda: gnorm(2)), (45 + INJD, lambda: gnorm(3))]
            if NP2:
                inject.append((3, lambda: norm_pair_b(1, 7)))
            inject.sort(key=lambda e: e[0])
            if CPINJ:
                inject = [(1, lambda: conv_chunk(0)), (7, lambda: conv_chunk(1)), (13, lambda: pool_chunk(0)),
                          (21, lambda: pool_chunk(1))] + inject
            t.phase = 'p4_mla'
            steps = []
            for kb in range(j):
                for h in range(4):
                    for kt in range(4):
                        steps.append((kb, h, kt, 0))
            for h in range(4):
                for kt in range(4):
                    steps.append((-1, h, kt, kt * 128))
            kvslot = {}
            sbank = {}
            first_seen = set()

            def emitS(i):
                kb, h, kt, q0 = steps[i]
                if kb >= 0:
                    if kb not in kvslot:
                        sl = st["kvl"] % 2
                        st["kvl"] += 1
                        kvslot[kb] = sl
                        t.dma("sp", f"KB{sl}", ktb[sl][0:96, :, :], kt_d[l][:, :, kb * 512:(kb + 1) * 512],
                              reads=[("ktd", l, kb)], writes=[("ktb", sl)])
                        t.dma("sp", f"VB{sl}", vb[sl][:, :, :], v_d[l][4 * kb:4 * kb + 4].rearrange("t p c -> p t c"),
                              reads=[("vd", l, kb)], writes=[("vb", sl)])
                    sl = kvslot[kb]
                    lhsT = ktb[sl][0:96, h, kt * 128:(kt + 1) * 128]
                    rk = [("ktb", sl)]
                else:
                    lhsT = ktc[0:96, h, kt * 128:(kt + 1) * 128]
                    rk = [("ktc", h)]
                b = nextbank(hold=True)
                sbank[i] = b
                mm(ps[b][:, q0:512], lhsT, qT[0:96, h, q0:512], True, True, rk + [("qT", h)], ("ps", b))

            def emitRest(i):
                kb, h, kt, q0 = steps[i]
                b = sbank[i]
                pi = st["pt"] % 3
                st["pt"] += 1
                actf(pt[pi][:, q0:512], ps[b][:, q0:512], AF.Exp, [("ps", b)], [("pt", pi)], scale=scale_mla)
                held.discard(b)
                if kb < 0:
                    tt(pt[pi][:, q0:q0 + 128], pt[pi][:, q0:q0 + 128], tri_bf[:, :], ALU.mult,
                       [("pt", pi), "tri_bf"], [("pt", pi)])
                    vk = "vc"
                    vap = vc[:, kt, :]
                else:
                    sl = kvslot[kb]
                    vk = ("vb", sl)
                    vap = vb[sl][:, kt, :]
                vcol = (h // 2) * 192 + (h % 2) * 64
                ob = 4 + h
                first = h not in first_seen
                first_seen.add(h)
                last = (kb < 0 and kt == 3)
                mm(ps[ob][:, q0:512], vap[:, vcol:vcol + 128], pt[pi][:, q0:512], first, last,
                   [("pt", pi), vk], ("ps", ob))

            emitS(0)
            if len(steps) > 1:
                emitS(1)
            nst = len(steps)
            for i in range(nst):
                if i + 2 < nst:
                    emitS(i + 2)
                emitRest(i)
                t.phase = 'p4_mla'
                if HOIST and inject and i >= inject[0][0]:
                    inject.pop(0)[1]()
                    t.phase = 'p4_mla'
                if NPMID and steps[i][0] < 0 and steps[i][1] == 1 and steps[i][2] == 3:
                    if NP2:
                        norm_pair_a(0, 0, [zero_col[0:64, :], zero_col[64:128, :]])
                        inject.insert(0, (i + 4, lambda: norm_pair_b(0, 0)))
                    else:
                        norm_pair(0, 0, [zero_col[0:64, :], zero_col[64:128, :]])
            if not NPMID:
                norm_pair(0, 0, [zero_col[0:64, :], zero_col[64:128, :]])
            norm_pair(1, 1, [zero_col[0:64, :], zero_col[64:128, :]])
            while inject:
                inject.pop(0)[1]()
            gnorm(0)
            t.phase = 'p9_wo'
            wo_first = 4 if WOSPLIT else 0
            if WOSPLIT:
                wslots = [load_slab(l, 18 + m) for m in range(4)]
                wbanks = [nextbank(hold=True) for m in range(4)]
                for m in range(4):
                    for i_, kc in enumerate([2, 3, 4, 5, 6, 7]):
                        mm(ps[wbanks[m]][:, :], wring[wslots[m]][:, kc * 128:(kc + 1) * 128], hb[:, kc, :],
                           i_ == 0, False, [("wr", wslots[m]), ("hb", kc)], ("ps", wbanks[m]))
                for m in range(4):
                    for kc in (0, 1):
                        mm(ps[wbanks[m]][:, :], wring[wslots[m]][:, kc * 128:(kc + 1) * 128], hb[:, kc, :],
                           False, kc == 1, [("wr", wslots[m]), ("hb", kc)], ("ps", wbanks[m]))
                    tt(cur["xT"][:, m, :], cur["xT"][:, m, :], ps[wbanks[m]][:, :], ALU.add,
                       [("x", cur["xi"], m), ("ps", wbanks[m])], [("x", cur["xi"], m)])
                    held.discard(wbanks[m])
            for m in range(wo_first, 8):
                slot = load_slab(l, 18 + m); b = nextbank()
                mm_slab8o(slot, 0, 128, b, 128, [2, 3, 4, 5, 6, 7, 0, 1])
                tt(cur["xT"][:, m, :], cur["xT"][:, m, :], ps[b][:, :], ALU.add,
                   [("x", cur["xi"], m), ("ps", b)], [("x", cur["xi"], m)])
                if INCR:
                    norm_sq(m, act[:, m, :], ("act", m))
                    if m >= 1:
                        norm_stat(m - 1, act[:, m - 1, :], ("act", m - 1), m == 1, False)
            if INCR:
                norm_stat(7, act[:, 7, :], ("act", 7), False, True)
            t.phase = 'p10_fnorm'
            if not FFNRS:
                if not INCR:
                    for c in range(8):
                        norm_sq_chunk(c, act[:, c, :], ("act", c), c == 0, c == 7)
                norm_finish(vb0 + 8)
            else:
                for c in range(8):
                    ts(hb[:, c, :], cur["xT"][:, c, :], vecs[:, vb0 + 8 + c: vb0 + 9 + c], ALU.mult,
                       [("x", cur["xi"], c), "vecs"], [("hb", c)])
                    norm_sq(c, act[:, c, :], ("act", c))
            t.phase = 'p11_gu'
            for i in range(22):
                slot = load_slab(l, 26 + 2 * i); bg = nextbank()
                mm_slab8(slot, 0, 128, bg, 128)
                if FFNRS and i == 0:
                    for c in range(8):
                        norm_stat(c, act[:, c, :], ("act", c), c == 0, c == 7)
                    actf(rs[0][:, :], ps[SB][:, :], AF.Ln, [("ps", SB)], ["rs0"], scale=1.0 / D, bias=eps_col)
                    actf(rs[0][:, :], rs[0][:, :], AF.Exp, ["rs0"], ["rs0"], scale=-0.5)
                slot = load_slab(l, 27 + 2 * i); bu = nextbank()
                mm_slab8(slot, 0, 128, bu, 128)
                si = st["sg"] % 2
                st["sg"] += 1
                if not FFNRS:
                    actf(sg[si][:, :], ps[bg][:, :], AF.Silu, [("ps", bg)], [("sg", si)])
                    tt(act[:, i, :], sg[si][:, :], ps[bu][:, :], ALU.mult, [("sg", si), ("ps", bu)], [("act", i)])
                else:
                    tt(sg[si][:, :], ps[bg][:, :], rs[0][:, :], ALU.mult, [("ps", bg), "rs0"], [("sg", si)])
                    actf(sg[si][:, :], sg[si][:, :], AF.Silu, [("sg", si)], [("sg", si)])
                    tt(sg[si][:, :], sg[si][:, :], ps[bu][:, :], ALU.mult, [("sg", si), ("ps", bu)], [("sg", si)])
                    tt(act[:, i, :], sg[si][:, :], rs[0][:, :], ALU.mult, [("sg", si), "rs0"], [("act", i)])
            t.phase = 'p12_down'
            for m in range(8):
                b = nextbank()
                for sub in range(3):
                    slot = load_slab(l, 70 + 3 * m + sub)
                    nk = 8 if sub < 2 else 6
                    for kk in range(nk):
                        kc = sub * 8 + kk
                        mm(ps[b][:, :], wring[slot][:, kk * 128:(kk + 1) * 128], act[:, kc, :],
                           kc == 0, kc == 21, [("wr", slot), ("act", kc)], ("ps", b))
                tt(cur["xT"][:, m, :], cur["xT"][:, m, :], ps[b][:, :], ALU.add,
                   [("x", cur["xi"], m), ("ps", b)], [("x", cur["xi"], m)])
                if INCR:
                    norm_sq(m, hb[:, m, :], ("hb", m))
                    if m >= 1:
                        norm_stat(m - 1, hb[:, m - 1, :], ("hb", m - 1), m == 1, False)
            if INCR:
                norm_stat(7, hb[:, 7, :], ("hb", 7), False, True)
        t.phase = 'p13_final'
        if not INCR:
            for c in range(8):
                norm_sq_chunk(c, hb[:, c, :], ("hb", c), c == 0, c == 7)
        norm_finish(L * VL, final=True)
        t.dma("pool", "OUT", out_v[:, :, tok0:tok0 + 512], y[:, :, :], reads=[("y", c) for c in range(8)])
    t.wait_all("pool")
    t.replay()
    nc._mk_tags = t.tags
    nc._mk_shapes = t.shapes
    return nc


def _slab(Wcols):
    K, mw = Wcols.shape
    kc = K // 128
    a = Wcols.reshape(kc, 128, mw).transpose(1, 0, 2).reshape(128, kc * mw)
    out = np.zeros((128, 1024), np.float32)
    out[:, :kc * mw] = a
    return out


def _pack_weights(w_in, w_uq, w_ukv, pool_w, w_o, w_gate_up, w_down):
    packs = np.zeros((L * NSLAB, 128, 1024), np.float32)
    perm = np.concatenate([np.arange(16, 32), np.arange(0, 16)])
    for l in range(L):
        sl = []
        wi = w_in[l]
        sl.append(_slab(wi[:, 0:128]))
        sl.append(_slab(wi[:, 128:256]))
        sl.append(_slab(wi[:, 256:384]))
        kr = wi[:, 384:416]
        sl.append(_slab(np.concatenate([np.zeros((D, 64), np.float32), kr, kr[:, perm]], axis=1)))
        for base in (416, 672, 928, 1184):
            sl.append(_slab(wi[:, base:base + 128]))
            sl.append(_slab(wi[:, base + 128:base + 256]))
        qs = wi[:, 1440:1696]
        sl.append(_slab(np.concatenate([qs[:, 0:64], qs[:, 128:192]], axis=1)))
        sl.append(_slab(np.concatenate([qs[:, 64:128], qs[:, 192:256]], axis=1)))
        sl.append(_slab(wi[:, 1696:1824]))
        sl.append(_slab(wi[:, 1824:1952]))
        wq = w_uq[l]
        blocks = []
        for h in range(4):
            hq = wq[:, 96 * h:96 * h + 96]
            blocks.append(np.concatenate([hq[:, 0:64], hq[:, 64:96], hq[:, 64:96][:, perm]], axis=1))
        sl.append(_slab(np.concatenate(blocks, axis=1)))
        wk = w_ukv[l]
        knope = np.concatenate([wk[:, 128 * h:128 * h + 64] for h in range(4)], axis=1)
        vv = np.concatenate([wk[:, 128 * h + 64:128 * h + 128] for h in range(4)], axis=1)
        bds = []
        for c in range(2):
            bd = np.zeros((128, 128), np.float32)
            bd[0:64, 0:64] = pool_w[l, 2 * c]
            bd[64:128, 64:128] = pool_w[l, 2 * c + 1]
            bds.append(bd)
        sl.append(_slab(np.concatenate([knope, vv] + bds, axis=1)))
        for m in range(8):
            sl.append(_slab(w_o[l][:, 128 * m:128 * m + 128]))
        for i in range(22):
            sl.append(_slab(w_gate_up[l][:, 128 * i:128 * i + 128]))
            sl.append(_slab(w_gate_up[l][:, DFF + 128 * i:DFF + 128 * i + 128]))
        for m in range(8):
            col = w_down[l][:, 128 * m:128 * m + 128]
            sl.append(_slab(col[0:1024]))
            sl.append(_slab(col[1024:2048]))
            sl.append(_slab(col[2048:2816]))
        assert len(sl) == NSLAB
        packs[l * NSLAB:(l + 1) * NSLAB] = np.stack(sl)
    return packs


def _pack_vecs(attn_norm, ffn_norm, mix_norm, mla_q_norm, mla_kv_norm, conv_w, pool_scale, swa_sinks, final_norm):
    v = np.zeros((128, NV), np.float32)

    def cols(vec):
        return np.asarray(vec, np.float32).reshape(-1, 128).T

    for l in range(L):
        b = l * VL
        v[:, b + 0:b + 8] = cols(attn_norm[l])
        v[:, b + 8:b + 16] = cols(ffn_norm[l])
        v[:, b + 16:b + 24] = cols(mix_norm[l])
        v[:, b + 24:b + 26] = cols(mla_q_norm[l])
        v[:, b + 26:b + 27] = cols(mla_kv_norm[l])
        for c in range(2):
            for k in range(3):
                v[:, b + 27 + c * 3 + k] = conv_w[l, k, c * 128:(c + 1) * 128]
        v[:, b + 33:b + 35] = cols(pool_scale[l])
        v[:, b + 35:b + 39] = np.broadcast_to(np.asarray(swa_sinks[l], np.float32)[None, :], (128, 4))
    v[:, L * VL:L * VL + 8] = cols(final_norm)
    return v


def _constants(S):
    c = np.zeros((128, NCST), np.float32)
    c[:, 0:128] = 1.0
    p = np.arange(128)[:, None]
    f = np.arange(128)[None, :]
    c[:, 128:256] = (p <= f).astype(np.float32)
    for k in range(128):
        c[k, 256 + (k + 64) % 128] = 1.0
    slopes = [2.0 ** (-8.0 * (i + 1) / 4) for i in range(4)]
    k = np.arange(128)[:, None].astype(np.float64)
    q = np.arange(128)[None, :].astype(np.float64)
    for h in range(4):
        dist_prev = q + 128 - k
        m_prev = np.where(dist_prev < 128, np.exp(-slopes[h] * dist_prev), 0.0)
        dist_cur = q - k
        m_cur = np.where(dist_cur >= 0, np.exp(-slopes[h] * dist_cur), 0.0)
        c[:, 384 + h * 256: 384 + h * 256 + 128] = m_prev
        c[:, 384 + h * 256 + 128: 384 + h * 256 + 256] = m_cur
    wins = (2, 4, 8, 16)
    for ch in range(2):
        for half in range(2):
            w = wins[2 * ch + half]
            rows = slice(half * 64, half * 64 + 64)
            c[rows, 1408 + ch] = 1.0 / w
            pos = np.arange(16)
            c[rows, 1410 + ch * 16: 1410 + (ch + 1) * 16] = (w / np.minimum(pos + 1, w))[None, :]
    c[:, 1442] = EPS
    c[:, 1443] = 0.0
    inv = 1.0 / (10000.0 ** (np.arange(0, 32, 2, dtype=np.float32) / 32))
    ang = np.arange(S, dtype=np.float32)[:, None] * inv[None, :].astype(np.float32)
    cos = np.cos(ang).astype(np.float32).T
    sin = np.sin(ang).astype(np.float32).T
    rope = np.zeros((2, 32, S), np.float32)
    rope[0, 0:16] = cos
    rope[0, 16:32] = cos
    rope[1, 0:16] = -sin
    rope[1, 16:32] = sin
    return c, rope


def host_pack(inputs):
    f = lambda k: np.asarray(inputs[k], np.float32)
    wpk = _pack_weights(f("w_in"), f("w_uq"), f("w_ukv"), f("pool_w"), f("w_o"), f("w_gate_up"), f("w_down"))
    vecs = _pack_vecs(f("attn_norm"), f("ffn_norm"), f("mix_norm"), f("mla_q_norm"), f("mla_kv_norm"),
                      f("conv_w"), f("pool_scale"), f("swa_sinks"), f("final_norm"))
    return wpk, vecs


def kernel(**inputs):
    x = np.asarray(inputs["x"], np.float32)
    B, S, _ = x.shape
    NB = S // 512
    wpk, vecs = host_pack(inputs)
    cst, rope = _constants(S)
    nc = build(NB)
    in_maps = []
    for b in range(B):
        in_maps.append({"xT": np.ascontiguousarray(x[b].T), "wpk": wpk, "vecs": vecs, "cst": cst, "rope": rope})
    res = run_bass_kernel_spmd(nc, in_maps, core_ids=list(range(B)))
    out = np.stack([np.ascontiguousarray(r["outT"].T) for r in res.results], axis=0)
    return out.astype(np.float32)
```

```python
import math
import os
import numpy as np
import concourse.bass as bass
import concourse.mybir as mybir
from concourse.bass_utils import run_bass_kernel_spmd

F32 = mybir.dt.float32
BF16 = mybir.dt.bfloat16
AF = mybir.ActivationFunctionType
ALU = mybir.AluOpType

L = 2
D = 1024
DFF = 2816
NSLAB = 94
VL = 39
NV = L * VL + 8
NCST = 1444
EPS = 1e-6
NWR = 8
ENGS = ("pe", "act", "dve", "pool", "sp")


class Tracker:
    def __init__(self, nc):
        self.nc = nc
        self.ops = {e: [] for e in ENGS}
        self.sems = {}
        self.cnt = {}
        self.waited = {e: {} for e in ENGS}
        self.lastw = {}
        self.readers = {}
        self.phase = 'pro'
        self.shapes = []
        self.tags = {e: [] for e in ENGS}
        for e in ENGS:
            self._sem("E_" + e)

    def _sem(self, name):
        if name not in self.sems:
            self.sems[name] = self.nc.alloc_semaphore(name=name)
            self.cnt[name] = 0
        return name

    def _deps(self, eng, reads, writes):
        need = {}
        own = "E_" + eng

        def add(tok, same_ok):
            s, v = tok
            if s == own and same_ok:
                return
            if v > need.get(s, 0):
                need[s] = v

        for k in reads:
            tok = self.lastw.get(k)
            if tok is not None:
                add(tok, eng == "pe")
        for k in writes:
            tok = self.lastw.get(k)
            if tok is not None:
                add(tok, eng == "pe")
            for s, v in self.readers.get(k, {}).items():
                add((s, v), eng == "pe")
        out = []
        w = self.waited[eng]
        for s, v in need.items():
            if w.get(s, 0) < v:
                w[s] = v
                out.append((s, v))
        return out

    def _emit_waits(self, eng, waits):
        for s, v in waits:
            h = self.sems[s]
            self.ops[eng].append(lambda e, h=h, v=v: e.wait_ge(h, v))

    def _record(self, tok, reads, writes):
        s, v = tok
        for k in reads:
            d = self.readers.setdefault(k, {})
            if d.get(s, 0) < v:
                d[s] = v
        for k in writes:
            self.lastw[k] = tok
            self.readers[k] = {}

    def op(self, eng, fn, reads=(), writes=(), signal=True):
        self._emit_waits(eng, self._deps(eng, reads, writes))
        self.tags[eng].append(self.phase)
        own = "E_" + eng
        if signal:
            self.cnt[own] += 1
            h = self.sems[own]
            self.ops[eng].append(lambda e, fn=fn, h=h: fn(e).then_inc(h, 1))
            tok = (own, self.cnt[own])
        else:
            self.ops[eng].append(lambda e, fn=fn: fn(e))
            tok = (own, self.cnt[own] + 1)
        self._record(tok, reads, writes)
        return tok

    def dma(self, eng, semname, out_ap, in_ap, reads=(), writes=()):
        self._sem(semname)
        waits = self._deps(eng, reads, writes)
        if self.cnt[semname] > self.waited[eng].get(semname, 0):
            self.waited[eng][semname] = self.cnt[semname]
            waits.append((semname, self.cnt[semname]))
        self._emit_waits(eng, waits)
        self.cnt[semname] += 16
        h = self.sems[semname]
        self.ops[eng].append(lambda e, o=out_ap, i=in_ap, h=h: e.dma_start(out=o, in_=i).then_inc(h, 16))
        tok = (semname, self.cnt[semname])
        self._record(tok, reads, writes)
        return tok

    def wait_all(self, eng):
        for s, v in self.cnt.items():
            if v > 0 and self.waited[eng].get(s, 0) < v:
                self.waited[eng][s] = v
                self._emit_waits(eng, [(s, v)])

    def replay(self):
        ops = self.ops
        with self.nc.Block() as block:
            @block.tensor
            def _(e):
                for f in ops["pe"]:
                    f(e)

            @block.scalar
            def _(e):
                for f in ops["act"]:
                    f(e)

            @block.vector
            def _(e):
                for f in ops["dve"]:
                    f(e)

            @block.gpsimd
            def _(e):
                for f in ops["pool"]:
                    f(e)

            @block.sync
            def _(e):
                for f in ops["sp"]:
                    f(e)


def build(NB):
    S = NB * 512
    nc = bass.Bass("TRN2", target_bir_lowering=False)
    xT_d = nc.dram_tensor("xT", [D, S], F32, kind="ExternalInput").ap()
    wpk_d = nc.dram_tensor("wpk", [L * NSLAB, 128, 1024], F32, kind="ExternalInput").ap()
    vecs_d = nc.dram_tensor("vecs", [128, NV], F32, kind="ExternalInput").ap()
    cst_d = nc.dram_tensor("cst", [128, NCST], F32, kind="ExternalInput").ap()
    rope_d = nc.dram_tensor("rope", [2, 32, S], F32, kind="ExternalInput").ap()
    out_d = nc.dram_tensor("outT", [D, S], F32, kind="ExternalOutput").ap()
    wbf_d = nc.dram_tensor("wbf", [L * NSLAB, 128, 1024], BF16, kind="Internal").ap()
    kt_d = nc.dram_tensor("ktd", [L, 96, 4, S], BF16, kind="Internal").ap()
    v_d = nc.dram_tensor("vd", [L, S // 128, 128, 384], BF16, kind="Internal").ap()

    xT_v = xT_d.rearrange("(c p) s -> p c s", p=128)
    out_v = out_d.rearrange("(c p) s -> p c s", p=128)

    A = nc.alloc_sbuf_tensor
    xTs = [A("xT_sb0", [128, 8, 512], F32), A("xT_sb1", [128, 8, 512], F32)]
    hb = A("hb", [128, 8, 512], BF16)
    act = A("act", [128, 22, 512], BF16)
    y = A("y", [128, 8, 512], F32)
    wring = [A(f"wr{i}", [128, 1024], BF16) for i in range(NWR)]
    rs = [A(f"rs{i}", [128, 512], F32) for i in range(2)]
    cq = A("cq", [128, 2, 512], F32)
    ckv = A("ckv", [128, 512], F32)
    cqn = A("cqn", [128, 2, 512], BF16)
    ckvn = A("ckvn", [128, 512], BF16)
    gb = A("gb", [128, 2, 512], F32)
    gc = A("gc", [128, 2, 512], F32)
    z = A("z", [128, 2, 514], F32)
    zh = [A(f"zh{l}", [128, 2, 2], F32) for l in range(L)]
    u = A("u", [128, 2, 528], F32)
    uh = [A(f"uh{l}", [128, 2, 16], F32) for l in range(L)]
    pa = A("pa", [128, 528], F32)
    pb = A("pb", [128, 528], F32)
    pooled = A("pooled", [128, 2, 512], BF16)
    qsw = A("qsw", [128, 2, 512], BF16)
    ksw = [A(f"ksw{l}", [128, 8 * 128], BF16) for l in range(L)]
    vsw = [A(f"vsw{l}", [128, 8, 384], BF16) for l in range(L)]
    qT = A("qT", [128, 4, 512], BF16)
    ktc = A("ktc", [128, 4, 512], BF16)
    vc = A("vc", [128, 4, 384], BF16)
    ktb = [A(f"ktb{i}", [128, 4, 512], BF16) for i in range(2)]
    vb = [A(f"vb{i}", [128, 4, 384], BF16) for i in range(2)]
    pt = [A(f"pt{i}", [128, 512], BF16) for i in range(3)]
    ctab = A("ctab", [128, 512], F32)
    stab = A("stab", [128, 512], F32)
    rt1 = A("rt1", [128, 512], F32)
    rt2 = A("rt2", [128, 512], F32)
    esw = [A(f"esw{i}", [128, 512], F32) for i in range(2)]
    psw = [A(f"psw{i}", [128, 512], BF16) for i in range(2)]
    rtt = A("rtt", [128, 512], F32)
    sg = [A(f"sg{i}", [128, 512], F32) for i in range(2)]
    vecs = A("vecs_sb", [128, NV], F32)
    cst = A("cst_sb", [128, NCST], F32)
    esink = A("esink", [128, L * 4], F32)
    ones_bf = A("ones_bf", [128, 128], BF16)
    poolw = A("poolw", [128, L, 256], BF16)
    swap_bf = A("swap_bf", [128, 128], BF16)
    rtb = [A(f"rtb{i}", [128, 512], BF16) for i in range(2)]
    tri_bf = A("tri_bf", [128, 128], BF16)
    ps = [nc.alloc_psum_tensor(f"ps{i}", [128, 512], F32) for i in range(8)]
    print("sbuf bytes remaining:", nc.sbuf_bytes_remaining)

    swapm = cst[:, 256:384]
    eps_col = cst[:, 1442:1443]
    zero_col = cst[:, 1443:1444]

    def swamask(h, c0, c1):
        return cst[:, 384 + h * 256 + c0: 384 + h * 256 + c1]

    t = Tracker(nc)
    cur = {"xT": xTs[0], "xi": 0}
    st = {"bank": 0, "wslab": 0, "kvl": 0, "pt": 0, "sw": 0, "sg": 0, "np": 0, "j": 0}

    held = set()

    def nextbank(hold=False):
        for _ in range(4):
            b = st["bank"] % 4
            st["bank"] += 1
            if b not in held:
                if hold:
                    held.add(b)
                return b
        raise RuntimeError("no free PSUM ring bank")

    def mm(out, lhsT, rhs, start, stop, reads, wkey):
        t.op("pe", lambda e: e.matmul(out, lhsT=lhsT, rhs=rhs, start=start, stop=stop),
             reads=reads, writes=[wkey], signal=stop)
        t.shapes.append((lhsT.shape[0], lhsT.shape[-1], rhs.shape[-1], str(lhsT.dtype)))

    def actf(out, in_, func, reads, writes, scale=1.0, bias=0.0):
        t.op("act", lambda e: e.activation(out=out, in_=in_, func=func, bias=bias, scale=scale),
             reads=reads, writes=writes)

    def tt(out, in0, in1, op, reads, writes, eng="dve"):
        t.op(eng, lambda e: e.tensor_tensor(out=out, in0=in0, in1=in1, op=op), reads=reads, writes=writes)

    def ts(out, in0, s1, op0, reads, writes, eng="dve"):
        t.op(eng, lambda e: e.tensor_scalar(out=out, in0=in0, scalar1=s1, scalar2=None, op0=op0),
             reads=reads, writes=writes)

    def stt(out, in0, scalar, in1, op0, op1, reads, writes, eng="dve"):
        t.op(eng, lambda e: e.scalar_tensor_tensor(out=out, in0=in0, scalar=scalar, in1=in1, op0=op0, op1=op1),
             reads=reads, writes=writes)

    def recip(out, in_, reads, writes):
        t.op("dve", lambda e: e.reciprocal(out=out, in_=in_), reads=reads, writes=writes)

    def cp(out, in_, reads, writes, eng="dve"):
        t.op(eng, lambda e: e.tensor_copy(out=out, in_=in_), reads=reads, writes=writes)

    def mset(ap, val, writes, eng="dve"):
        t.op(eng, lambda e: e.memset(ap, val), writes=writes)

    def load_slab(l, s):
        g = st["wslab"]
        st["wslab"] += 1
        slot = g % NWR
        if st["j"] == 0:
            t.dma("pool", f"WC{slot}", wring[slot][:, :], wpk_d[l * NSLAB + s], writes=[("wr", slot)])
            t.dma("sp", f"WS{g % 4}", wbf_d[l * NSLAB + s], wring[slot][:, :], reads=[("wr", slot)],
                  writes=[("wbf", l, s)])
        else:
            t.dma("sp", f"W{slot}", wring[slot][:, :], wbf_d[l * NSLAB + s],
                  reads=[("wbf", l, s)], writes=[("wr", slot)])
        return slot

    def stat_rstd(sq_list, Dn, rsbuf, rskey):
        b = nextbank()
        n = len(sq_list)
        for i, (ap, k) in enumerate(sq_list):
            mm(ps[b][:, :], ones_bf[:, :], ap, i == 0, i == n - 1, [k, "ones_bf"], ("ps", b))
        actf(rsbuf[:, :], ps[b][:, :], AF.Ln, [("ps", b)], [rskey], scale=1.0 / Dn, bias=eps_col)
        actf(rsbuf[:, :], rsbuf[:, :], AF.Exp, [rskey], [rskey], scale=-0.5)

    def rmsnorm_x(gbase, out_is_final=False):
        sql = []
        for c in range(8):
            actf(act[:, c, :], cur["xT"][:, c, :], AF.Square, [("x", cur["xi"], c)], [("act", c)])
            sql.append((act[:, c, :], ("act", c)))
        stat_rstd(sql, D, rs[0], "rs0")
        for c in range(8):
            if out_is_final:
                stt(y[:, c, :], cur["xT"][:, c, :], vecs[:, gbase + c: gbase + c + 1], rs[0][:, :], ALU.mult, ALU.mult,
                    [("x", cur["xi"], c), "rs0", "vecs"], [("y", c)])
            else:
                stt(hb[:, c, :], cur["xT"][:, c, :], vecs[:, gbase + c: gbase + c + 1], rs[0][:, :], ALU.mult, ALU.mult,
                    [("x", cur["xi"], c), "rs0", "vecs"], [("hb", c)])

    def mm_slab8(slot, mlo, mhi, bank, out_rows, extra_reads=()):
        for kc in range(8):
            mm(ps[bank][0:out_rows, :], wring[slot][:, kc * 128 + mlo: kc * 128 + mhi], hb[:, kc, :],
               kc == 0, kc == 7, [("wr", slot), ("hb", kc)], ("ps", bank))

    npk = {}

    def norm_pair_a(p, ychunk, biases):
        he, ho = 4 + 2 * p, 4 + 2 * p + 1
        yk = ("y", ychunk)
        k = st["np"] % 2
        st["np"] += 1
        npk[ychunk] = k
        actf(rtt[0:64, :], ps[ho][0:64, :], AF.Ln, [("ps", ho)], ["rtt"], bias=biases[0])
        actf(rtt[64:128, :], ps[he][64:128, :], AF.Ln, [("ps", he)], ["rtt"], bias=biases[1])
        actf(rtb[k][:, :], rtt[:, :], AF.Exp, ["rtt"], [("rtb", k)], scale=-1.0)
        cp(y[0:64, ychunk, :], ps[he][0:64, :], [("ps", he)], [yk])
        cp(y[64:128, ychunk, :], ps[ho][64:128, :], [("ps", ho)], [yk])

    def norm_pair_b(p, ychunk):
        yk = ("y", ychunk)
        k = npk[ychunk]
        b = nextbank()
        mm(ps[b][:, :], swap_bf[:, :], rtb[k][:, :], True, True, ["swap_bf", ("rtb", k)], ("ps", b))
        tt(y[:, ychunk, :], y[:, ychunk, :], ps[b][:, :], ALU.mult, [yk, ("ps", b)], [yk])

    def norm_pair(p, ychunk, biases):
        norm_pair_a(p, ychunk, biases)
        norm_pair_b(p, ychunk)

    t.dma("sp", "CST", cst[:, :], cst_d, writes=["cst"])
    t.dma("sp", "VEC", vecs[:, :], vecs_d, writes=["vecs"])
    t.dma("pool", "XLD0", xTs[0][:, :, :], xT_v[:, :, 0:512], writes=[("x", 0, c) for c in range(8)])
    cp(ones_bf[:, :], cst[:, 0:128], ["cst"], ["ones_bf"])
    cp(tri_bf[:, :], cst[:, 128:256], ["cst"], ["tri_bf"])
    cp(swap_bf[:, :], cst[:, 256:384], ["cst"], ["swap_bf"])
    for l in range(L):
        actf(esink[:, l * 4:(l + 1) * 4], vecs[:, l * VL + 35: l * VL + 39], AF.Exp, ["vecs"], ["esink"])
        mset(zh[l][:, :, :], 0.0, [("zh", l)])
        mset(uh[l][:, :, :], 0.0, [("uh", l)])
        mset(vsw[l][:, :, :], 1.0, [("vsw", l)])
    mset(vc[:, :, :], 1.0, ["vc"])

    scale_mla = 1.0 / math.sqrt(96.0)

    INCR = os.environ.get('MK_INCR', '0') == '1'
    HOIST = os.environ.get('MK_INJ', '1') == '1'
    CP_ENG = os.environ.get('MK_CPENG', 'dve')
    INJ0 = int(os.environ.get('MK_INJ0', '9'))
    SWLA = int(os.environ.get('MK_SWLA', '2'))
    CPINJ = os.environ.get('MK_CPINJ', '1') == '1'
    ILV = int(os.environ.get('MK_ILV', '1'))
    INJD = int(os.environ.get('MK_INJD', '0'))
    FFNRS = os.environ.get('MK_FFNRS', '1') == '1'
    GN2 = os.environ.get('MK_GN2', '1') == '1'
    NP2 = os.environ.get('MK_NP2', '0') == '1'
    WOSPLIT = os.environ.get('MK_WOSPLIT', '1') == '1'
    ILV_A0 = int(os.environ.get('MK_ILVA0', '6'))
    NPMID = os.environ.get('MK_NPMID', '0') == '1'
    HOISTPREP = os.environ.get('MK_HOISTPREP', '0') == '1'
    SB = 4

    def norm_sq(c, sq_ap, sq_key):
        actf(sq_ap, cur["xT"][:, c, :], AF.Square, [("x", cur["xi"], c)], [sq_key])

    def norm_stat(c, sq_ap, sq_key, first, last):
        mm(ps[SB][:, :], ones_bf[:, :], sq_ap, first, last, [sq_key, "ones_bf"], ("ps", SB))

    def norm_sq_chunk(c, sq_ap, sq_key, first, last):
        norm_sq(c, sq_ap, sq_key)
        norm_stat(c, sq_ap, sq_key, first, last)

    def norm_finish(gbase, final=False):
        actf(rs[0][:, :], ps[SB][:, :], AF.Ln, [("ps", SB)], ["rs0"], scale=1.0 / D, bias=eps_col)
        actf(rs[0][:, :], rs[0][:, :], AF.Exp, ["rs0"], ["rs0"], scale=-0.5)
        for c in range(8):
            dst, dk = (y[:, c, :], ("y", c)) if final else (hb[:, c, :], ("hb", c))
            stt(dst, cur["xT"][:, c, :], vecs[:, gbase + c: gbase + c + 1], rs[0][:, :], ALU.mult, ALU.mult,
                [("x", cur["xi"], c), "rs0", "vecs"], [dk])

    def mm_slab8o(slot, mlo, mhi, bank, out_rows, order):
        for i, kc in enumerate(order):
            mm(ps[bank][0:out_rows, :], wring[slot][:, kc * 128 + mlo: kc * 128 + mhi], hb[:, kc, :],
               i == 0, i == len(order) - 1, [("wr", slot), ("hb", kc)], ("ps", bank))

    for j in range(NB):
        tok0 = j * 512
        st["j"] = j
        cur["xi"] = j % 2
        cur["xT"] = xTs[j % 2]
        if j + 1 < NB:
            nx = (j + 1) % 2
            t.dma("pool", f"XLD{nx}", xTs[nx][:, :, :], xT_v[:, :, tok0 + 512:tok0 + 1024],
                  writes=[("x", nx, c) for c in range(8)])
        t.dma("sp", "ROPE", ctab[64:96, :], rope_d[0, :, tok0:tok0 + 512], writes=["ctab"])
        t.dma("sp", "ROPE", stab[64:96, :], rope_d[1, :, tok0:tok0 + 512], writes=["stab"])
        for l in range(L):
            vb0 = l * VL
            par8 = (j % 2) * 4
            t.phase = 'p1_norm'
            if l == 0 or not INCR:
                for c in range(8):
                    norm_sq_chunk(c, hb[:, c, :], ("hb", c), c == 0, c == 7)
            norm_finish(vb0 + 0)
            t.phase = 'p2_win'
            for c in range(2):
                slot = load_slab(l, c); b = nextbank()
                mm_slab8(slot, 0, 128, b, 128)
                actf(cq[:, c, :], ps[b][:, :], AF.Copy, [("ps", b)], [("cq", c)])
                actf(act[:, c, :], cq[:, c, :], AF.Square, [("cq", c)], [("act", c)])
            slot = load_slab(l, 2); b = nextbank()
            mm_slab8(slot, 0, 128, b, 128)
            actf(ckv[:, :], ps[b][:, :], AF.Copy, [("ps", b)], ["ckv"])
            actf(act[:, 2, :], ckv[:, :], AF.Square, ["ckv"], [("act", 2)])
            slot = load_slab(l, 3)
            bA = nextbank()
            mm_slab8(slot, 0, 96, bA, 96)
            bB = nextbank()
            mm_slab8(slot, 32, 128, bB, 96)
            tt(rt1[64:96, :], ps[bA][64:96, :], ctab[64:96, :], ALU.mult, [("ps", bA), "ctab"], ["rt1"])
            tt(rt2[64:96, :], ps[bB][64:96, :], stab[64:96, :], ALU.mult, [("ps", bB), "stab"], ["rt2"])
            for h in range(4):
                tt(ktc[64:96, h, :], rt1[64:96, :], rt2[64:96, :], ALU.add, ["rt1", "rt2"], [("ktc", h)])
            stat_rstd([(act[:, c, :], ("act", c)) for c in range(2)], 256.0, rs[1], "rs1")
            for c in range(2):
                stt(cqn[:, c, :], cq[:, c, :], vecs[:, vb0 + 24 + c: vb0 + 25 + c], rs[1][:, :], ALU.mult, ALU.mult,
                    [("cq", c), "rs1", "vecs"], [("cqn", c)])
            stat_rstd([(act[:, 2, :], ("act", 2))], 128.0, rs[0], "rs0")
            stt(ckvn[:, :], ckv[:, :], vecs[:, vb0 + 26: vb0 + 27], rs[0][:, :], ALU.mult, ALU.mult,
                ["ckv", "rs0", "vecs"], ["ckvn"])
            for c in range(2):
                slot = load_slab(l, 4 + c); b = nextbank()
                mm_slab8(slot, 0, 128, b, 128)
                actf(gb[:, c, :], ps[b][:, :], AF.Copy, [("ps", b)], [("gb", c)])
            def conv_chunk(c, eng=CP_ENG):
                t.phase = 'p5_conv'
                w0 = vecs[:, vb0 + 27 + c * 3 + 0: vb0 + 27 + c * 3 + 1]
                w1 = vecs[:, vb0 + 27 + c * 3 + 1: vb0 + 27 + c * 3 + 2]
                w2 = vecs[:, vb0 + 27 + c * 3 + 2: vb0 + 27 + c * 3 + 3]
                yk = ("y", 2 + c)
                ts(y[:, 2 + c, :], z[:, c, 2:514], w2, ALU.mult, [("z", c), "vecs"], [yk], eng=eng)
                if eng == "pool":
                    for (zoff, wk) in ((1, w1), (0, w0)):
                        ts(sg[c][:, :], z[:, c, zoff:zoff + 512], wk, ALU.mult, [("z", c), "vecs"], [("sg", c)], eng=eng)
                        tt(y[:, 2 + c, :], y[:, 2 + c, :], sg[c][:, :], ALU.add, [yk, ("sg", c)], [yk], eng=eng)
                else:
                    stt(y[:, 2 + c, :], z[:, c, 1:513], w1, y[:, 2 + c, :], ALU.mult, ALU.add, [("z", c), yk, "vecs"], [yk], eng=eng)
                    stt(y[:, 2 + c, :], z[:, c, 0:512], w0, y[:, 2 + c, :], ALU.mult, ALU.add, [("z", c), yk, "vecs"], [yk], eng=eng)
                tt(y[:, 2 + c, :], y[:, 2 + c, :], gb[:, c, :], ALU.mult, [yk, ("gb", c)], [yk], eng=eng)

            def pool_chunk(c, eng=CP_ENG):
                t.phase = 'p6a_pool'
                uk = ("u", c)
                tt(pa[:, 1:528], u[:, c, 1:528], u[:, c, 0:527], ALU.add, [uk], ["pa"], eng=eng)
                tt(pb[:, 3:528], pa[:, 3:528], pa[:, 1:526], ALU.add, ["pa"], ["pb"], eng=eng)
                if c == 1:
                    tt(pa[:, 7:528], pb[:, 7:528], pb[:, 3:524], ALU.add, ["pb"], ["pa"], eng=eng)
                    tt(pb[:, 15:528], pa[:, 15:528], pa[:, 7:520], ALU.add, ["pa"], ["pb"], eng=eng)
                for (r0, r1, src, sk) in ((0, 64, pa, "pa"), (64, 128, pb, "pb")):
                    if j == 0:
                        tt(src[r0:r1, 16:32], src[r0:r1, 16:32], cst[r0:r1, 1410 + c * 16: 1410 + (c + 1) * 16],
                           ALU.mult, [sk, "cst"], [sk], eng=eng)
                    if eng == "pool":
                        ts(src[r0:r1, 16:528], src[r0:r1, 16:528], cst[r0:r1, 1408 + c: 1409 + c], ALU.mult,
                           [sk, "cst"], [sk], eng=eng)
                        tt(pooled[r0:r1, c, :], src[r0:r1, 16:528], u[r0:r1, c, 16:528], ALU.subtract,
                           [sk, uk], [("pooled", c)], eng=eng)
                    else:
                        stt(pooled[r0:r1, c, :], src[r0:r1, 16:528], cst[r0:r1, 1408 + c: 1409 + c], u[r0:r1, c, 16:528],
                            ALU.mult, ALU.subtract, [sk, uk, "cst"], [("pooled", c)], eng=eng)

            def _prep():
                t.phase = 'p3_mlaprep'
                slot = load_slab(l, 16)
                for h in range(4):
                    bA = nextbank()
                    for kc in range(2):
                        o = kc * 512 + h * 128
                        mm(ps[bA][0:96, :], wring[slot][:, o:o + 96], cqn[:, kc, :], kc == 0, kc == 1,
                           [("wr", slot), ("cqn", kc)], ("ps", bA))
                    bB = nextbank()
                    for kc in range(2):
                        o = kc * 512 + h * 128
                        mm(ps[bB][0:96, :], wring[slot][:, o + 32:o + 128], cqn[:, kc, :], kc == 0, kc == 1,
                           [("wr", slot), ("cqn", kc)], ("ps", bB))
                    actf(qT[0:64, h, :], ps[bA][0:64, :], AF.Copy, [("ps", bA)], [("qT", h)])
                    tt(rt1[64:96, :], ps[bA][64:96, :], ctab[64:96, :], ALU.mult, [("ps", bA), "ctab"], ["rt1"])
                    tt(rt2[64:96, :], ps[bB][64:96, :], stab[64:96, :], ALU.mult, [("ps", bB), "stab"], ["rt2"])
                    tt(qT[64:96, h, :], rt1[64:96, :], rt2[64:96, :], ALU.add, ["rt1", "rt2"], [("qT", h)])
                    yield
                slot17 = load_slab(l, 17)
                if j == 0:
                    cp(poolw[:, l, :], wring[slot17][:, 512:768], [("wr", slot17)], [("poolw", l)])
                for h in range(4):
                    b = nextbank()
                    mm(ps[b][0:64, :], wring[slot17][:, h * 64:(h + 1) * 64], ckvn[:, :], True, True,
                       [("wr", slot17), "ckvn"], ("ps", b))
                    actf(ktc[0:64, h, :], ps[b][0:64, :], AF.Copy, [("ps", b)], [("ktc", h)])
                    if h % 2 == 1:
                        yield
                for half in range(2):
                    b = nextbank()
                    for t2 in range(2):
                        tt_ = half * 2 + t2
                        mm(ps[b][:, t2 * 256:(t2 + 1) * 256], ckvn[:, tt_ * 128:(tt_ + 1) * 128],
                           wring[slot17][:, 256:512], True, True, [("wr", slot17), "ckvn"], ("ps", b))
                    psv = ps[b][:, :].rearrange("p (t c) -> p t c", c=256)
                    for h in range(4):
                        col = (h // 2) * 192 + (h % 2) * 128
                        cp(vc[:, half * 2:half * 2 + 2, col:col + 64], psv[:, :, h * 64:(h + 1) * 64],
                           [("ps", b)], ["vc"])
                    yield
                if j + 1 < NB:
                    t.dma("pool", f"KST{l}", kt_d[l][:, :, tok0:tok0 + 512], ktc[0:96, :, :],
                          reads=[("ktc", h) for h in range(4)], writes=[("ktd", l, j)])
                    t.dma("pool", f"VST{l}", v_d[l][4 * j:4 * j + 4].rearrange("t p c -> p t c"), vc[:, :, :],
                          reads=["vc"], writes=[("vd", l, j)])
            def _p2c():
                t.phase = 'p2_win'
                for c in range(2):
                    slot = load_slab(l, 6 + c); b = nextbank()
                    mm_slab8(slot, 0, 128, b, 128)
                    actf(gc[:, c, :], ps[b][:, :], AF.Copy, [("ps", b)], [("gc", c)])
                    yield
                for c in range(2):
                    cp(z[:, c, 0:2], zh[l][:, c, :], [("zh", l)], [("z", c)])
                    slot = load_slab(l, 8 + c); b = nextbank()
                    mm_slab8(slot, 0, 128, b, 128)
                    tt(z[:, c, 2:514], ps[b][:, :], gc[:, c, :], ALU.mult, [("ps", b), ("gc", c)], [("z", c)])
                    cp(zh[l][:, c, :], z[:, c, 512:514], [("z", c)], [("zh", l)])
                    yield
                for c in range(2):
                    cp(u[:, c, 0:16], uh[l][:, c, :], [("uh", l)], [("u", c)])
                    slot = load_slab(l, 10 + c); b = nextbank()
                    mm_slab8(slot, 0, 128, b, 128)
                    actf(u[:, c, 16:528], ps[b][:, :], AF.Copy, [("ps", b)], [("u", c)])
                    cp(uh[l][:, c, :], u[:, c, 512:528], [("u", c)], [("uh", l)])
                    yield
                if not CPINJ:
                    conv_chunk(0); conv_chunk(1); pool_chunk(0); pool_chunk(1)
                t.phase = 'p2_win'
                for c in range(2):
                    slot = load_slab(l, 12 + c); b = nextbank()
                    mm_slab8(slot, 0, 128, b, 128)
                    actf(qsw[:, c, :], ps[b][:, :], AF.Copy, [("ps", b)], [("qsw", c)])
                    yield
                slot = load_slab(l, 14); b = nextbank()
                mm_slab8(slot, 0, 128, b, 128)
                actf(ksw[l][:, par8 * 128:(par8 + 4) * 128], ps[b][:, :], AF.Copy, [("ps", b)], [("ksw", l, j % 2)])
                yield
                slot = load_slab(l, 15); b = nextbank()
                for tt_ in range(4):
                    for kc in range(8):
                        mm(ps[b][:, tt_ * 128:(tt_ + 1) * 128], hb[:, kc, tt_ * 128:(tt_ + 1) * 128],
                           wring[slot][:, kc * 128:(kc + 1) * 128], kc == 0, kc == 7,
                           [("wr", slot), ("hb", kc)], ("ps", b))
                psv = ps[b][:, :].rearrange("p (t c) -> p t c", c=128)
                for kv in range(2):
                    for off in (0, 128):
                        cp(vsw[l][:, par8:par8 + 4, kv * 192 + off: kv * 192 + off + 64],
                           psv[:, :, kv * 64:(kv + 1) * 64], [("ps", b)], [("vsw", l)])
            gens = [_p2c(), _prep()]
            if ILV == 0:
                for g_ in gens:
                    for _ in g_:
                        pass
            else:
                alive = [True, True]
                first_a = ILV_A0
                while any(alive):
                    for gi_, g_ in enumerate(gens):
                        n_ = first_a if (gi_ == 0 and first_a) else 1
                        if gi_ == 0:
                            first_a = 0
                        for _ in range(max(n_, 1)):
                            if alive[gi_]:
                                try:
                                    next(g_)
                                except StopIteration:
                                    alive[gi_] = False
            t.phase = 'p7_swa'
            sw_steps = [(p, qt) for p in range(2) for qt in range(4)]
            sw_bank = {}

            def swS(i):
                p, qt = sw_steps[i]
                T = 4 * j + qt
                cur_s = T % 8
                prv_s = (T - 1) % 8
                r0 = p * 64
                b = nextbank(hold=True)
                sw_bank[i] = b
                for hh in range(2):
                    rk = [("ksw", l, 0), ("ksw", l, 1), ("qsw", hh)]
                    if T > 0:
                        mm(ps[b][:, hh * 256:hh * 256 + 128], ksw[l][r0:r0 + 64, prv_s * 128:(prv_s + 1) * 128],
                           qsw[r0:r0 + 64, hh, qt * 128:(qt + 1) * 128], True, True, rk, ("ps", b))
                    mm(ps[b][:, hh * 256 + 128:hh * 256 + 256], ksw[l][r0:r0 + 64, cur_s * 128:(cur_s + 1) * 128],
                       qsw[r0:r0 + 64, hh, qt * 128:(qt + 1) * 128], True, True, rk, ("ps", b))

            def swRest(i):
                p, qt = sw_steps[i]
                T = 4 * j + qt
                cur_s = T % 8
                prv_s = (T - 1) % 8
                b = sw_bank[i]
                si = st["sw"] % 2
                st["sw"] += 1
                rngs = [(0, 512)] if T > 0 else [(128, 256), (384, 512)]
                for (c0, c1) in rngs:
                    actf(esw[si][:, c0:c1], ps[b][:, c0:c1], AF.Exp, [("ps", b)], [("esw", si)], scale=0.125)
                    tt(psw[si][:, c0:c1], esw[si][:, c0:c1], cst[:, 384 + p * 512 + c0: 384 + p * 512 + c1], ALU.mult,
                       [("esw", si), "cst"], [("psw", si)])
                held.discard(b)
                for hh in range(2):
                    h = 2 * p + hh
                    vcol = p * 192 + hh * 64
                    ob = 4 + h
                    if T > 0:
                        mm(ps[ob][:, qt * 128:(qt + 1) * 128], vsw[l][:, prv_s, vcol:vcol + 128],
                           psw[si][:, hh * 256:hh * 256 + 128], True, False, [("vsw", l), ("psw", si)], ("ps", ob))
                    mm(ps[ob][:, qt * 128:(qt + 1) * 128], vsw[l][:, cur_s, vcol:vcol + 128],
                       psw[si][:, hh * 256 + 128:hh * 256 + 256], not (T > 0), True, [("vsw", l), ("psw", si)], ("ps", ob))

            for i0_ in range(SWLA):
                swS(i0_)
            for i in range(len(sw_steps)):
                if i + SWLA < len(sw_steps):
                    swS(i + SWLA)
                swRest(i)
                if i == 3 and NPMID:
                    norm_pair_a(0, 6, [esink[0:64, l * 4 + 1: l * 4 + 2], esink[64:128, l * 4 + 0: l * 4 + 1]])
                if i == 5 and NPMID:
                    norm_pair_b(0, 6)
            if not NPMID:
                norm_pair(0, 6, [esink[0:64, l * 4 + 1: l * 4 + 2], esink[64:128, l * 4 + 0: l * 4 + 1]])
            if NP2:
                norm_pair_a(1, 7, [esink[0:64, l * 4 + 3: l * 4 + 4], esink[64:128, l * 4 + 2: l * 4 + 3]])
            else:
                norm_pair(1, 7, [esink[0:64, l * 4 + 3: l * 4 + 4], esink[64:128, l * 4 + 2: l * 4 + 3]])

            def poolmm(c):
                t.phase = 'p6b_poolmm'
                b = nextbank()
                mm(ps[b][:, :], poolw[:, l, c * 128:(c + 1) * 128], pooled[:, c, :], True, True,
                   [("poolw", l), ("pooled", c)], ("ps", b))
                ts(y[:, 4 + c, :], ps[b][:, :], vecs[:, vb0 + 33 + c: vb0 + 34 + c], ALU.mult,
                   [("ps", b), "vecs"], [("y", 4 + c)])

            gbank = {}

            def gnorm_sq(g):
                t.phase = 'p8_gnorm'
                for c in range(2):
                    cc = 2 * g + c
                    tt(act[:, cc, :], y[:, cc, :], y[:, cc, :], ALU.mult, [("y", cc)], [("act", cc)])

            def gnorm_st(g):
                t.phase = 'p8_gnorm'
                b = nextbank(hold=True)
                gbank[g] = b
                for c in range(2):
                    cc = 2 * g + c
                    mm(ps[b][:, :], ones_bf[:, :], act[:, cc, :], c == 0, c == 1, [("act", cc), "ones_bf"], ("ps", b))

            def gnorm_a(g):
                gnorm_sq(g)
                gnorm_st(g)

            def gnorm_b(g):
                t.phase = 'p8_gnorm'
                b = gbank[g]
                rsb, rsk = (rs[g % 2], f"rs{g % 2}")
                actf(rsb[:, :], ps[b][:, :], AF.Ln, [("ps", b)], [rsk], scale=1.0 / 256.0, bias=eps_col)
                held.discard(b)
                actf(rsb[:, :], rsb[:, :], AF.Exp, [rsk], [rsk], scale=-0.5)
                for c in range(2):
                    cc = 2 * g + c
                    stt(hb[:, cc, :], y[:, cc, :], vecs[:, vb0 + 16 + cc: vb0 + 17 + cc], rsb[:, :], ALU.mult, ALU.mult,
                        [("y", cc), rsk, "vecs"], [("hb", cc)])

            def gnorm(g):
                gnorm_a(g)
                gnorm_b(g)

            if GN2:
                inject = [(27, lambda: gnorm_sq(1)), (29, lambda: poolmm(0)), (31, lambda: gnorm_st(1)), (33, lambda: gnorm_sq(3)),
                          (35, lambda: gnorm_b(1)), (37, lambda: gnorm_st(3)), (39, lambda: poolmm(1)), (41, lambda: gnorm_b(3)),
                          (43, lambda: poolmm(1) if False else None), (45, lambda: gnorm_sq(2)), (47, lambda: gnorm_st(2)),
                          (51, lambda: gnorm_b(2))]
                inject = [e for e in inject if e[0] != 43]
            else:
                inject = [(29 + INJD, lambda: poolmm(0)), (39 + INJD, lambda: poolmm(1)), (33 + INJD, lambda: gnorm(1)),
                          (41 + INJD, lambda: gnorm(2)), (45 + INJD, lambda: gnorm(3))]
            if NP2:
                inject.append((3, lambda: norm_pair_b(1, 7)))
            inject.sort(key=lambda e: e[0])
            if CPINJ:
                inject = [(1, lambda: conv_chunk(0)), (7, lambda: conv_chunk(1)), (13, lambda: pool_chunk(0)),
                          (21, lambda: pool_chunk(1))] + inject
            t.phase = 'p4_mla'
            steps = []
            for kb in range(j):
                for h in range(4):
                    for kt in range(4):
                        steps.append((kb, h, kt, 0))
            for h in range(4):
                for kt in range(4):
                    steps.append((-1, h, kt, kt * 128))
            kvslot = {}
            sbank = {}
            first_seen = set()

            def emitS(i):
                kb, h, kt, q0 = steps[i]
                if kb >= 0:
                    if kb not in kvslot:
                        sl = st["kvl"] % 2
                        st["kvl"] += 1
                        kvslot[kb] = sl
                        t.dma("sp", f"KB{sl}", ktb[sl][0:96, :, :], kt_d[l][:, :, kb * 512:(kb + 1) * 512],
                              reads=[("ktd", l, kb)], writes=[("ktb", sl)])
                        t.dma("sp", f"VB{sl}", vb[sl][:, :, :], v_d[l][4 * kb:4 * kb + 4].rearrange("t p c -> p t c"),
                              reads=[("vd", l, kb)], writes=[("vb", sl)])
                    sl = kvslot[kb]
                    lhsT = ktb[sl][0:96, h, kt * 128:(kt + 1) * 128]
                    rk = [("ktb", sl)]
                else:
                    lhsT = ktc[0:96, h, kt * 128:(kt + 1) * 128]
                    rk = [("ktc", h)]
                b = nextbank(hold=True)
                sbank[i] = b
                mm(ps[b][:, q0:512], lhsT, qT[0:96, h, q0:512], True, True, rk + [("qT", h)], ("ps", b))

            def emitRest(i):
                kb, h, kt, q0 = steps[i]
                b = sbank[i]
                pi = st["pt"] % 3
                st["pt"] += 1
                actf(pt[pi][:, q0:512], ps[b][:, q0:512], AF.Exp, [("ps", b)], [("pt", pi)], scale=scale_mla)
                held.discard(b)
                if kb < 0:
                    tt(pt[pi][:, q0:q0 + 128], pt[pi][:, q0:q0 + 128], tri_bf[:, :], ALU.mult,
                       [("pt", pi), "tri_bf"], [("pt", pi)])
                    vk = "vc"
                    vap = vc[:, kt, :]
                else:
                    sl = kvslot[kb]
                    vk = ("vb", sl)
                    vap = vb[sl][:, kt, :]
                vcol = (h // 2) * 192 + (h % 2) * 64
                ob = 4 + h
                first = h not in first_seen
                first_seen.add(h)
                last = (kb < 0 and kt == 3)
                mm(ps[ob][:, q0:512], vap[:, vcol:vcol + 128], pt[pi][:, q0:512], first, last,
                   [("pt", pi), vk], ("ps", ob))

            emitS(0)
            if len(steps) > 1:
                emitS(1)
            nst = len(steps)
            for i in range(nst):
                if i + 2 < nst:
                    emitS(i + 2)
                emitRest(i)
                t.phase = 'p4_mla'
                if HOIST and inject and i >= inject[0][0]:
                    inject.pop(0)[1]()
                    t.phase = 'p4_mla'
                if NPMID and steps[i][0] < 0 and steps[i][1] == 1 and steps[i][2] == 3:
                    if NP2:
                        norm_pair_a(0, 0, [zero_col[0:64, :], zero_col[64:128, :]])
                        inject.insert(0, (i + 4, lambda: norm_pair_b(0, 0)))
                    else:
                        norm_pair(0, 0, [zero_col[0:64, :], zero_col[64:128, :]])
            if not NPMID:
                norm_pair(0, 0, [zero_col[0:64, :], zero_col[64:128, :]])
            norm_pair(1, 1, [zero_col[0:64, :], zero_col[64:128, :]])
            while inject:
                inject.pop(0)[1]()
            gnorm(0)
            t.phase = 'p9_wo'
            wo_first = 4 if WOSPLIT else 0
            if WOSPLIT:
                wslots = [load_slab(l, 18 + m) for m in range(4)]
                wbanks = [nextbank(hold=True) for m in range(4)]
                for m in range(4):
                    for i_, kc in enumerate([2, 3, 4, 5, 6, 7]):
                        mm(ps[wbanks[m]][:, :], wring[wslots[m]][:, kc * 128:(kc + 1) * 128], hb[:, kc, :],
                           i_ == 0, False, [("wr", wslots[m]), ("hb", kc)], ("ps", wbanks[m]))
                for m in range(4):
                    for kc in (0, 1):
                        mm(ps[wbanks[m]][:, :], wring[wslots[m]][:, kc * 128:(kc + 1) * 128], hb[:, kc, :],
                           False, kc == 1, [("wr", wslots[m]), ("hb", kc)], ("ps", wbanks[m]))
                    tt(cur["xT"][:, m, :], cur["xT"][:, m, :], ps[wbanks[m]][:, :], ALU.add,
                       [("x", cur["xi"], m), ("ps", wbanks[m])], [("x", cur["xi"], m)])
                    held.discard(wbanks[m])
            for m in range(wo_first, 8):
                slot = load_slab(l, 18 + m); b = nextbank()
                mm_slab8o(slot, 0, 128, b, 128, [2, 3, 4, 5, 6, 7, 0, 1])
                tt(cur["xT"][:, m, :], cur["xT"][:, m, :], ps[b][:, :], ALU.add,
                   [("x", cur["xi"], m), ("ps", b)], [("x", cur["xi"], m)])
                if INCR:
                    norm_sq(m, act[:, m, :], ("act", m))
                    if m >= 1:
                        norm_stat(m - 1, act[:, m - 1, :], ("act", m - 1), m == 1, False)
            if INCR:
                norm_stat(7, act[:, 7, :], ("act", 7), False, True)
            t.phase = 'p10_fnorm'
            if not FFNRS:
                if not INCR:
                    for c in range(8):
                        norm_sq_chunk(c, act[:, c, :], ("act", c), c == 0, c == 7)
                norm_finish(vb0 + 8)
            else:
                for c in range(8):
                    ts(hb[:, c, :], cur["xT"][:, c, :], vecs[:, vb0 + 8 + c: vb0 + 9 + c], ALU.mult,
                       [("x", cur["xi"], c), "vecs"], [("hb", c)])
                    norm_sq(c, act[:, c, :], ("act", c))
            t.phase = 'p11_gu'
            for i in range(22):
                slot = load_slab(l, 26 + 2 * i); bg = nextbank()
                mm_slab8(slot, 0, 128, bg, 128)
                if FFNRS and i == 0:
                    for c in range(8):
                        norm_stat(c, act[:, c, :], ("act", c), c == 0, c == 7)
                    actf(rs[0][:, :], ps[SB][:, :], AF.Ln, [("ps", SB)], ["rs0"], scale=1.0 / D, bias=eps_col)
                    actf(rs[0][:, :], rs[0][:, :], AF.Exp, ["rs0"], ["rs0"], scale=-0.5)
                slot = load_slab(l, 27 + 2 * i); bu = nextbank()
                mm_slab8(slot, 0, 128, bu, 128)
                si = st["sg"] % 2
                st["sg"] += 1
                if not FFNRS:
                    actf(sg[si][:, :], ps[bg][:, :], AF.Silu, [("ps", bg)], [("sg", si)])
                    tt(act[:, i, :], sg[si][:, :], ps[bu][:, :], ALU.mult, [("sg", si), ("ps", bu)], [("act", i)])
                else:
                    tt(sg[si][:, :], ps[bg][:, :], rs[0][:, :], ALU.mult, [("ps", bg), "rs0"], [("sg", si)])
                    actf(sg[si][:, :], sg[si][:, :], AF.Silu, [("sg", si)], [("sg", si)])
                    tt(sg[si][:, :], sg[si][:, :], ps[bu][:, :], ALU.mult, [("sg", si), ("ps", bu)], [("sg", si)])
                    tt(act[:, i, :], sg[si][:, :], rs[0][:, :], ALU.mult, [("sg", si), "rs0"], [("act", i)])
            t.phase = 'p12_down'
            for m in range(8):
                b = nextbank()
                for sub in range(3):
                    slot = load_slab(l, 70 + 3 * m + sub)
                    nk = 8 if sub < 2 else 6
                    for kk in range(nk):
                        kc = sub * 8 + kk
                        mm(ps[b][:, :], wring[slot][:, kk * 128:(kk + 1) * 128], act[:, kc, :],
                           kc == 0, kc == 21, [("wr", slot), ("act", kc)], ("ps", b))
                tt(cur["xT"][:, m, :], cur["xT"][:, m, :], ps[b][:, :], ALU.add,
                   [("x", cur["xi"], m), ("ps", b)], [("x", cur["xi"], m)])
                if INCR:
                    norm_sq(m, hb[:, m, :], ("hb", m))
                    if m >= 1:
                        norm_stat(m - 1, hb[:, m - 1, :], ("hb", m - 1), m == 1, False)
            if INCR:
                norm_stat(7, hb[:, 7, :], ("hb", 7), False, True)
        t.phase = 'p13_final'
        if not INCR:
            for c in range(8):
                norm_sq_chunk(c, hb[:, c, :], ("hb", c), c == 0, c == 7)
        norm_finish(L * VL, final=True)
        t.dma("pool", "OUT", out_v[:, :, tok0:tok0 + 512], y[:, :, :], reads=[("y", c) for c in range(8)])
    t.wait_all("pool")
    t.replay()
    nc._mk_tags = t.tags
    nc._mk_shapes = t.shapes
    return nc


def _slab(Wcols):
    K, mw = Wcols.shape
    kc = K // 128
    a = Wcols.reshape(kc, 128, mw).transpose(1, 0, 2).reshape(128, kc * mw)
    out = np.zeros((128, 1024), np.float32)
    out[:, :kc * mw] = a
    return out


def _pack_weights(w_in, w_uq, w_ukv, pool_w, w_o, w_gate_up, w_down):
    packs = np.zeros((L * NSLAB, 128, 1024), np.float32)
    perm = np.concatenate([np.arange(16, 32), np.arange(0, 16)])
    for l in range(L):
        sl = []
        wi = w_in[l]
        sl.append(_slab(wi[:, 0:128]))
        sl.append(_slab(wi[:, 128:256]))
        sl.append(_slab(wi[:, 256:384]))
        kr = wi[:, 384:416]
        sl.append(_slab(np.concatenate([np.zeros((D, 64), np.float32), kr, kr[:, perm]], axis=1)))
        for base in (416, 672, 928, 1184):
            sl.append(_slab(wi[:, base:base + 128]))
            sl.append(_slab(wi[:, base + 128:base + 256]))
        qs = wi[:, 1440:1696]
        sl.append(_slab(np.concatenate([qs[:, 0:64], qs[:, 128:192]], axis=1)))
        sl.append(_slab(np.concatenate([qs[:, 64:128], qs[:, 192:256]], axis=1)))
        sl.append(_slab(wi[:, 1696:1824]))
        sl.append(_slab(wi[:, 1824:1952]))
        wq = w_uq[l]
        blocks = []
        for h in range(4):
            hq = wq[:, 96 * h:96 * h + 96]
            blocks.append(np.concatenate([hq[:, 0:64], hq[:, 64:96], hq[:, 64:96][:, perm]], axis=1))
        sl.append(_slab(np.concatenate(blocks, axis=1)))
        wk = w_ukv[l]
        knope = np.concatenate([wk[:, 128 * h:128 * h + 64] for h in range(4)], axis=1)
        vv = np.concatenate([wk[:, 128 * h + 64:128 * h + 128] for h in range(4)], axis=1)
        bds = []
        for c in range(2):
            bd = np.zeros((128, 128), np.float32)
            bd[0:64, 0:64] = pool_w[l, 2 * c]
            bd[64:128, 64:128] = pool_w[l, 2 * c + 1]
            bds.append(bd)
        sl.append(_slab(np.concatenate([knope, vv] + bds, axis=1)))
        for m in range(8):
            sl.append(_slab(w_o[l][:, 128 * m:128 * m + 128]))
        for i in range(22):
            sl.append(_slab(w_gate_up[l][:, 128 * i:128 * i + 128]))
            sl.append(_slab(w_gate_up[l][:, DFF + 128 * i:DFF + 128 * i + 128]))
        for m in range(8):
            col = w_down[l][:, 128 * m:128 * m + 128]
            sl.append(_slab(col[0:1024]))
            sl.append(_slab(col[1024:2048]))
            sl.append(_slab(col[2048:2816]))
        assert len(sl) == NSLAB
        packs[l * NSLAB:(l + 1) * NSLAB] = np.stack(sl)
    return packs


def _pack_vecs(attn_norm, ffn_norm, mix_norm, mla_q_norm, mla_kv_norm, conv_w, pool_scale, swa_sinks, final_norm):
    v = np.zeros((128, NV), np.float32)

    def cols(vec):
        return np.asarray(vec, np.float32).reshape(-1, 128).T

    for l in range(L):
        b = l * VL
        v[:, b + 0:b + 8] = cols(attn_norm[l])
        v[:, b + 8:b + 16] = cols(ffn_norm[l])
        v[:, b + 16:b + 24] = cols(mix_norm[l])
        v[:, b + 24:b + 26] = cols(mla_q_norm[l])
        v[:, b + 26:b + 27] = cols(mla_kv_norm[l])
        for c in range(2):
            for k in range(3):
                v[:, b + 27 + c * 3 + k] = conv_w[l, k, c * 128:(c + 1) * 128]
        v[:, b + 33:b + 35] = cols(pool_scale[l])
        v[:, b + 35:b + 39] = np.broadcast_to(np.asarray(swa_sinks[l], np.float32)[None, :], (128, 4))
    v[:, L * VL:L * VL + 8] = cols(final_norm)
    return v


def _constants(S):
    c = np.zeros((128, NCST), np.float32)
    c[:, 0:128] = 1.0
    p = np.arange(128)[:, None]
    f = np.arange(128)[None, :]
    c[:, 128:256] = (p <= f).astype(np.float32)
    for k in range(128):
        c[k, 256 + (k + 64) % 128] = 1.0
    slopes = [2.0 ** (-8.0 * (i + 1) / 4) for i in range(4)]
    k = np.arange(128)[:, None].astype(np.float64)
    q = np.arange(128)[None, :].astype(np.float64)
    for h in range(4):
        dist_prev = q + 128 - k
        m_prev = np.where(dist_prev < 128, np.exp(-slopes[h] * dist_prev), 0.0)
        dist_cur = q - k
        m_cur = np.where(dist_cur >= 0, np.exp(-slopes[h] * dist_cur), 0.0)
        c[:, 384 + h * 256: 384 + h * 256 + 128] = m_prev
        c[:, 384 + h * 256 + 128: 384 + h * 256 + 256] = m_cur
    wins = (2, 4, 8, 16)
    for ch in range(2):
        for half in range(2):
            w = wins[2 * ch + half]
            rows = slice(half * 64, half * 64 + 64)
            c[rows, 1408 + ch] = 1.0 / w
            pos = np.arange(16)
            c[rows, 1410 + ch * 16: 1410 + (ch + 1) * 16] = (w / np.minimum(pos + 1, w))[None, :]
    c[:, 1442] = EPS
    c[:, 1443] = 0.0
    inv = 1.0 / (10000.0 ** (np.arange(0, 32, 2, dtype=np.float32) / 32))
    ang = np.arange(S, dtype=np.float32)[:, None] * inv[None, :].astype(np.float32)
    cos = np.cos(ang).astype(np.float32).T
    sin = np.sin(ang).astype(np.float32).T
    rope = np.zeros((2, 32, S), np.float32)
    rope[0, 0:16] = cos
    rope[0, 16:32] = cos
    rope[1, 0:16] = -sin
    rope[1, 16:32] = sin
    return c, rope


def host_pack(inputs):
    f = lambda k: np.asarray(inputs[k], np.float32)
    wpk = _pack_weights(f("w_in"), f("w_uq"), f("w_ukv"), f("pool_w"), f("w_o"), f("w_gate_up"), f("w_down"))
    vecs = _pack_vecs(f("attn_norm"), f("ffn_norm"), f("mix_norm"), f("mla_q_norm"), f("mla_kv_norm"),
                      f("conv_w"), f("pool_scale"), f("swa_sinks"), f("final_norm"))
    return wpk, vecs


def kernel(**inputs):
    x = np.asarray(inputs["x"], np.float32)
    B, S, _ = x.shape
    NB = S // 512
    wpk, vecs = host_pack(inputs)
    cst, rope = _constants(S)
    nc = build(NB)
    in_maps = []
    for b in range(B):
        in_maps.append({"xT": np.ascontiguousarray(x[b].T), "wpk": wpk, "vecs": vecs, "cst": cst, "rope": rope})
    res = run_bass_kernel_spmd(nc, in_maps, core_ids=list(range(B)))
    out = np.stack([np.ascontiguousarray(r["outT"].T) for r in res.results], axis=0)
    return out.astype(np.float32)
```
